# Optimizing a Trainium2 kernel written in Bass

```python
import jax
import jax.numpy as jnp
from jax import lax
import numpy as np

D_MODEL = 2048
BATCH = 1
SEQ = 8192
DEPTH = 2

CTX_LEN = 256
GRID_W = 64

GDN_HEADS = D_MODEL // 256
GDN_HEAD_DIM = 128
GDN_WIDTH = GDN_HEADS * GDN_HEAD_DIM
GDN_CONV = 3
GDN_CHUNK = 64
RWKV_HEAD_DIM = 64
RWKV_HEADS = D_MODEL // 128
RWKV_WIDTH = RWKV_HEADS * RWKV_HEAD_DIM
RWKV_DECAY_RANK = 64
RWKV_A_RANK = 64
RWKV_GATE_RANK = 160

D_MIX = GDN_WIDTH + RWKV_WIDTH
D_FF = 4 * D_MODEL
N_MOD = 6
NORM_EPS = 1e-6
RWKV_LN_EPS = 64e-5

GDN_SIZES = (3 * GDN_WIDTH, GDN_WIDTH, GDN_HEADS, GDN_HEADS, GDN_HEADS, GDN_HEADS)
RWKV_SIZES = (RWKV_WIDTH, RWKV_WIDTH, RWKV_WIDTH,
              RWKV_DECAY_RANK, RWKV_DECAY_RANK, RWKV_A_RANK, RWKV_A_RANK, RWKV_GATE_RANK)
P_GDN = sum(GDN_SIZES)
P_RWKV = sum(RWKV_SIZES)
P_IN = P_GDN + P_RWKV

kernel_name = "hybrid_gdn_rwkv7_prefix_dit_block"


def split_cols(t, sizes):
    return jnp.split(t, [int(s) for s in np.cumsum(sizes)[:-1]], axis=-1)


def rmsnorm(x, g):
    xf = x.astype(jnp.float32)
    y = xf * lax.rsqrt(jnp.mean(xf * xf, axis=-1, keepdims=True) + NORM_EPS)
    return (y * g.astype(jnp.float32)).astype(x.dtype)


def l2norm(t):
    tf = t.astype(jnp.float32)
    return tf * lax.rsqrt(jnp.sum(tf * tf, axis=-1, keepdims=True) + 1e-6)


def neighbour_mean(t, axis):
    n = t.shape[axis]
    pad = [(0, 0)] * t.ndim
    pad[axis] = (1, 1)
    tp = jnp.pad(t, pad)
    prev = lax.slice_in_dim(tp, 0, n, axis=axis)
    nxt = lax.slice_in_dim(tp, 2, n + 2, axis=axis)
    return 0.5 * (prev + nxt)


def centred_dwconv(t, w):
    k = w.shape[0]
    p = k // 2
    n = t.shape[1]
    tp = jnp.pad(t, ((0, 0), (p, p), (0, 0)))
    out = tp[:, 0:n] * w[0]
    for j in range(1, k):
        out = out + tp[:, j:j + n] * w[j]
    return out


def to_col_major(t, rows):
    b, n, c = t.shape
    return t.reshape(b, rows, GRID_W, c).transpose(0, 2, 1, 3).reshape(b, n, c)


def to_row_major(t, rows):
    b, n, c = t.shape
    return t.reshape(b, GRID_W, rows, c).transpose(0, 2, 1, 3).reshape(b, n, c)


def maybe_flip(t, d, axis):
    return jnp.flip(t, axis=axis) if d == 1 else t


def gdn_chunked(q, k, v, log_decay, beta, s0):
    b, h, t, _ = q.shape
    dv = v.shape[-1]
    c = GDN_CHUNK
    n = t // c
    q, k, v = (a.reshape(b, h, n, c, a.shape[-1]) for a in (q, k, v))
    beta = beta.reshape(b, h, n, c)
    g = jnp.cumsum(log_decay.reshape(b, h, n, c), axis=-1)
    incl = jnp.tril(jnp.ones((c, c), dtype=bool))
    strict = jnp.tril(jnp.ones((c, c), dtype=bool), -1)
    decay = jnp.where(incl, jnp.exp(jnp.where(incl, g[..., :, None] - g[..., None, :], 0.0)), 0.0)
    k_beta = k * beta[..., None]
    l_mat = jnp.where(strict, jnp.einsum('bhnik,bhnjk->bhnij', k_beta, k) * decay, 0.0)
    eye = jnp.eye(c, dtype=q.dtype)
    t_inv = lax.linalg.triangular_solve(l_mat + eye, jnp.broadcast_to(eye, l_mat.shape),
                                        left_side=True, lower=True, unit_diagonal=True)
    u = jnp.einsum('bhnij,bhnjv->bhniv', t_inv, v * beta[..., None])
    w = jnp.einsum('bhnij,bhnjk->bhnik', t_inv, k_beta * jnp.exp(g)[..., None])
    a_intra = jnp.where(incl, jnp.einsum('bhnik,bhnjk->bhnij', q, k) * decay, 0.0)
    g_last = g[..., -1]
    q_e = q * jnp.exp(g)[..., None]
    k_e = k * jnp.exp(g_last[..., None] - g)[..., None]

    def step(s, inp):
        q_i, k_i, u_i, w_i, a_i, dl_i = inp
        v_new = u_i - jnp.einsum('bhik,bhkv->bhiv', w_i, s)
        o = jnp.einsum('bhik,bhkv->bhiv', q_i, s) + jnp.einsum('bhij,bhjv->bhiv', a_i, v_new)
        s = s * dl_i[..., None, None] + jnp.einsum('bhik,bhiv->bhkv', k_i, v_new)
        return s, o

    xs = tuple(jnp.moveaxis(a, 2, 0) for a in (q_e, k_e, u, w, a_intra, jnp.exp(g_last)))
    s, o = lax.scan(step, s0, xs)
    return s, jnp.moveaxis(o, 0, 2).reshape(b, h, t, dv)


def gdn_prepare(p, conv_w, a_log, dt_bias):
    b, t, _ = p.shape
    qkv, z, b_f, b_b, al_f, al_b = split_cols(p, GDN_SIZES)
    qkv = jax.nn.silu(centred_dwconv(qkv, conv_w))
    q, k, v = jnp.split(qkv, 3, axis=-1)

    def heads(a):
        return a.reshape(b, t, GDN_HEADS, GDN_HEAD_DIM).transpose(0, 2, 1, 3).astype(jnp.float32)

    q = l2norm(heads(q)) * (GDN_HEAD_DIM ** -0.5)
    k = l2norm(heads(k))
    v = heads(v)
    betas = jax.nn.sigmoid(jnp.stack([b_f, b_b]).astype(jnp.float32)).transpose(0, 1, 3, 2)
    alphas = jnp.stack([al_f, al_b]).astype(jnp.float32)
    log_decay = -jnp.exp(a_log.astype(jnp.float32))[:, None, None, :] * jax.nn.softplus(
        alphas + dt_bias.astype(jnp.float32)[:, None, None, :])
    return q, k, v, z, betas, log_decay.transpose(0, 1, 3, 2)


def gdn_output(o, z, norm_g):
    b, h, t, _ = o.shape
    o = rmsnorm(o, norm_g).transpose(0, 2, 1, 3).reshape(b, t, GDN_WIDTH)
    return o * jax.nn.silu(z.astype(jnp.float32))


def gdn_group(p_l, p_c, conv_w, a_log, dt_bias, norm_g):
    q_l, k_l, v_l, z_l, b_l, g_l = gdn_prepare(p_l, conv_w, a_log, dt_bias)
    q_c, k_c, v_c, z_c, b_c, g_c = gdn_prepare(p_c, conv_w, a_log, dt_bias)
    s0 = jnp.zeros((p_l.shape[0], GDN_HEADS, GDN_HEAD_DIM, GDN_HEAD_DIM), jnp.float32)
    outs_l, outs_c = [], []
    for d in range(2):
        s_c, oc = gdn_chunked(maybe_flip(q_c, d, 2), maybe_flip(k_c, d, 2), maybe_flip(v_c, d, 2),
                              maybe_flip(g_c[d], d, 2), maybe_flip(b_c[d], d, 2), s0)
        _, ol = gdn_chunked(maybe_flip(q_l, d, 2), maybe_flip(k_l, d, 2), maybe_flip(v_l, d, 2),
                            maybe_flip(g_l[d], d, 2), maybe_flip(b_l[d], d, 2), s_c)
        outs_c.append(maybe_flip(oc, d, 2))
        outs_l.append(maybe_flip(ol, d, 2))
    return (gdn_output(outs_l[0] + outs_l[1], z_l, norm_g),
            gdn_output(outs_c[0] + outs_c[1], z_c, norm_g))


def rwkv7_scan(r, decay, k, v, kk, a, s0):
    def step(s, inp):
        r_t, w_t, k_t, v_t, kk_t, a_t = inp
        sa = jnp.einsum('bhvk,bhk->bhv', s, -kk_t)
        s = (s * w_t[:, :, None, :] + sa[..., None] * (kk_t * a_t)[:, :, None, :]
             + v_t[..., None] * k_t[:, :, None, :])
        return s, jnp.einsum('bhvk,bhk->bhv', s, r_t)

    xs = tuple(jnp.moveaxis(t, 1, 0) for t in (r, decay, k, v, kk, a))
    s, out = lax.scan(step, s0, xs)
    return s, jnp.moveaxis(out, 0, 1)


def rwkv_prepare(p, w0, w2, a0, a2, g2, k_k, k_a):
    b, t, _ = p.shape
    r, k, v, w1f, w1b, a1f, a1b, g1 = split_cols(p.astype(jnp.float32), RWKV_SIZES)

    def heads(a):
        return a.reshape(b, t, RWKV_HEADS, RWKV_HEAD_DIM)

    gate = jax.nn.sigmoid(g1) @ g2
    kk = l2norm(heads(k * k_k))
    dirs = []
    for d, (w1, a1) in enumerate(((w1f, a1f), (w1b, a1b))):
        w = -jax.nn.softplus(-(w0[d] + jnp.tanh(w1) @ w2[d])) - 0.5
        decay = jnp.exp(-jnp.exp(w))
        a = jax.nn.sigmoid(a0[d] + a1 @ a2[d])
        k_d = k * (1.0 + (a - 1.0) * k_a)
        dirs.append((heads(decay), heads(k_d), heads(a)))
    return heads(r), heads(v), kk, gate, dirs


def rwkv_output(o, r, v, ks, gate, r_k, ln_g, ln_b):
    b, t, h, n = o.shape
    mean = jnp.mean(o, axis=-1, keepdims=True)
    var = jnp.var(o, axis=-1, keepdims=True)
    o = ((o - mean) * lax.rsqrt(var + RWKV_LN_EPS)).reshape(b, t, RWKV_WIDTH) * ln_g + ln_b
    rk = r_k.reshape(h, n)
    bonus = (jnp.sum(r * ks[0] * rk, axis=-1, keepdims=True)
             + jnp.sum(r * ks[1] * rk, axis=-1, keepdims=True)) * v
    return (o + bonus.reshape(b, t, RWKV_WIDTH)) * gate


def rwkv_group(p_l, p_c, rows, mu, w0, w2, a0, a2, g2, k_k, k_a, r_k, ln_g, ln_b):
    b, n, c = p_l.shape
    p_l = p_l + mu * (neighbour_mean(p_l.reshape(b, rows, GRID_W, c), 1).reshape(b, n, c) - p_l)
    p_l = to_col_major(p_l, rows)
    p_c = p_c + mu * (neighbour_mean(p_c, 1) - p_c)
    r_l, v_l, kk_l, gate_l, dirs_l = rwkv_prepare(p_l, w0, w2, a0, a2, g2, k_k, k_a)
    r_c, v_c, kk_c, gate_c, dirs_c = rwkv_prepare(p_c, w0, w2, a0, a2, g2, k_k, k_a)
    s0 = jnp.zeros((b, RWKV_HEADS, RWKV_HEAD_DIM, RWKV_HEAD_DIM), jnp.float32)
    outs_l, outs_c = [], []
    for d in range(2):
        dec_c, kd_c, a_c = dirs_c[d]
        dec_l, kd_l, a_l = dirs_l[d]
        s_c, oc = rwkv7_scan(maybe_flip(r_c, d, 1), maybe_flip(dec_c, d, 1), maybe_flip(kd_c, d, 1),
                             maybe_flip(v_c, d, 1), maybe_flip(kk_c, d, 1), maybe_flip(a_c, d, 1), s0)
        _, ol = rwkv7_scan(maybe_flip(r_l, d, 1), maybe_flip(dec_l, d, 1), maybe_flip(kd_l, d, 1),
                           maybe_flip(v_l, d, 1), maybe_flip(kk_l, d, 1), maybe_flip(a_l, d, 1), s_c)
        outs_c.append(maybe_flip(oc, d, 1))
        outs_l.append(maybe_flip(ol, d, 1))
    out_l = rwkv_output(outs_l[0] + outs_l[1], r_l, v_l, (dirs_l[0][1], dirs_l[1][1]), gate_l, r_k, ln_g, ln_b)
    out_c = rwkv_output(outs_c[0] + outs_c[1], r_c, v_c, (dirs_c[0][1], dirs_c[1][1]), gate_c, r_k, ln_g, ln_b)
    return to_row_major(out_l, rows), out_c


def hybrid_mixer(h_l, h_c, rows, need_ctx, w_in, conv_w, a_log, dt_bias, gdn_norm_g, mu, w0, w2, a0, a2,
                 g2, k_k, k_a, r_k, ln_g, ln_b, w_out):
    p_l = h_l @ w_in
    p_c = h_c @ w_in
    og_l, og_c = gdn_group(p_l[..., :P_GDN], p_c[..., :P_GDN], conv_w, a_log, dt_bias, gdn_norm_g)
    or_l, or_c = rwkv_group(p_l[..., P_GDN:], p_c[..., P_GDN:], rows, mu, w0, w2, a0, a2, g2,
                            k_k, k_a, r_k, ln_g, ln_b)
    out_l = jnp.concatenate([og_l, or_l], axis=-1).astype(h_l.dtype) @ w_out
    out_c = jnp.concatenate([og_c, or_c], axis=-1).astype(h_c.dtype) @ w_out if need_ctx else None
    return out_l, out_c


def squared_relu_mlp(h, w1, w2):
    return jnp.square(jax.nn.relu(h @ w1)) @ w2


def setup_inputs(seed: int = 0) -> dict:
    key = jax.random.key(seed)
    ks = jax.random.split(key, 28)
    f32 = jnp.float32
    D = D_MODEL

    def nrm(k, shape, scale):
        return jax.random.normal(k, shape, f32) * scale

    dt = jnp.exp(jax.random.uniform(ks[11], (DEPTH, 2, GDN_HEADS), f32, float(np.log(1e-3)), float(np.log(1e-1))))
    dt_bias = dt + jnp.log(-jnp.expm1(-dt))
    return {
        "x": nrm(ks[0], (BATCH, SEQ, D), 1.0),
        "c": nrm(ks[1], (BATCH, D), 1.0),
        "ctx": nrm(ks[2], (BATCH, CTX_LEN, D), 1.0),
        "c_ctx": nrm(ks[3], (D,), 1.0),
        "ada_w": nrm(ks[4], (DEPTH, D, N_MOD * D), 0.5 * D ** -0.5),
        "ada_b": nrm(ks[5], (DEPTH, N_MOD * D), 0.02),
        "norm1_g": 1.0 + nrm(ks[6], (DEPTH, D), 0.05),
        "norm2_g": 1.0 + nrm(ks[7], (DEPTH, D), 0.05),
        "w_in": nrm(ks[8], (DEPTH, D, P_IN), D ** -0.5),
        "gdn_conv_w": nrm(ks[9], (DEPTH, GDN_CONV, 3 * GDN_WIDTH), GDN_CONV ** -0.5),
        "gdn_a_log": jnp.log(jax.random.uniform(ks[10], (DEPTH, 2, GDN_HEADS), f32, 1.0, 16.0)),
        "gdn_dt_bias": dt_bias,
        "gdn_norm_g": 1.0 + nrm(ks[12], (DEPTH, GDN_HEAD_DIM), 0.05),
        "rwkv_mu": jax.random.uniform(ks[13], (DEPTH, P_RWKV), f32, 0.0, 1.0),
        "rwkv_w0": jax.random.uniform(ks[14], (DEPTH, 2, RWKV_WIDTH), f32, -6.0, -1.0),
        "rwkv_w2": nrm(ks[15], (DEPTH, 2, RWKV_DECAY_RANK, RWKV_WIDTH), 0.1),
        "rwkv_a0": nrm(ks[16], (DEPTH, 2, RWKV_WIDTH), 0.1),
        "rwkv_a2": nrm(ks[17], (DEPTH, 2, RWKV_A_RANK, RWKV_WIDTH), RWKV_A_RANK ** -0.5),
        "rwkv_g2": nrm(ks[18], (DEPTH, RWKV_GATE_RANK, RWKV_WIDTH), RWKV_GATE_RANK ** -0.5),
        "rwkv_k_k": 0.85 + nrm(ks[19], (DEPTH, RWKV_WIDTH), 0.05),
        "rwkv_k_a": 1.0 + nrm(ks[20], (DEPTH, RWKV_WIDTH), 0.05),
        "rwkv_r_k": nrm(ks[21], (DEPTH, RWKV_WIDTH), 0.1),
        "rwkv_ln_g": 1.0 + nrm(ks[22], (DEPTH, RWKV_WIDTH), 0.05),
        "rwkv_ln_b": nrm(ks[23], (DEPTH, RWKV_WIDTH), 0.02),
        "w_out": nrm(ks[24], (DEPTH, D_MIX, D), D_MIX ** -0.5),
        "mlp_w1": nrm(ks[25], (DEPTH, D, D_FF), D ** -0.5),
        "mlp_w2": nrm(ks[26], (DEPTH, D_FF, D), D_FF ** -0.5),
        "final_g": 1.0 + nrm(ks[27], (D,), 0.05),
    }


def reference(x, c, ctx, c_ctx, ada_w, ada_b, norm1_g, norm2_g, w_in, gdn_conv_w, gdn_a_log, gdn_dt_bias,
              gdn_norm_g, rwkv_mu, rwkv_w0, rwkv_w2, rwkv_a0, rwkv_a2, rwkv_g2, rwkv_k_k, rwkv_k_a, rwkv_r_k,
              rwkv_ln_g, rwkv_ln_b, w_out, mlp_w1, mlp_w2, final_g):
    rows = x.shape[1] // GRID_W
    lat, cx = x, ctx
    for l in range(DEPTH):
        need_ctx = l < DEPTH - 1
        mod_l = jax.nn.silu(c) @ ada_w[l] + ada_b[l]
        mod_c = jax.nn.silu(c_ctx) @ ada_w[l] + ada_b[l]
        sh1_l, sc1_l, gt1_l, sh2_l, sc2_l, gt2_l = [m[:, None, :] for m in jnp.split(mod_l, N_MOD, axis=-1)]
        sh1_c, sc1_c, gt1_c, sh2_c, sc2_c, gt2_c = jnp.split(mod_c, N_MOD, axis=-1)
        h_l = rmsnorm(lat, norm1_g[l]) * (1 + sc1_l) + sh1_l
        h_c = rmsnorm(cx, norm1_g[l]) * (1 + sc1_c) + sh1_c
        mix_l, mix_c = hybrid_mixer(h_l, h_c, rows, need_ctx, w_in[l], gdn_conv_w[l], gdn_a_log[l],
                                    gdn_dt_bias[l], gdn_norm_g[l], rwkv_mu[l], rwkv_w0[l], rwkv_w2[l],
                                    rwkv_a0[l], rwkv_a2[l], rwkv_g2[l], rwkv_k_k[l], rwkv_k_a[l],
                                    rwkv_r_k[l], rwkv_ln_g[l], rwkv_ln_b[l], w_out[l])
        lat = lat + gt1_l * mix_l
        h2_l = rmsnorm(lat, norm2_g[l]) * (1 + sc2_l) + sh2_l
        lat = lat + gt2_l * squared_relu_mlp(h2_l, mlp_w1[l], mlp_w2[l])
        if need_ctx:
            cx = cx + gt1_c * mix_c
            h2_c = rmsnorm(cx, norm2_g[l]) * (1 + sc2_c) + sh2_c
            cx = cx + gt2_c * squared_relu_mlp(h2_c, mlp_w1[l], mlp_w2[l])
    return rmsnorm(lat, final_g)
```

```python
from contextlib import ExitStack
import numpy as np
import concourse.bass as bass
import concourse.mybir as mybir

F32 = mybir.dt.float32
BF16 = mybir.dt.bfloat16
AF = mybir.ActivationFunctionType
ALU = mybir.AluOpType
AX = mybir.AxisListType

EPOCH = 4096
NDMASEM = 8


F32R = mybir.dt.float32r
F32R_NAMES = {"identr", "gqk", "gqe", "rth", "rBt", "rKt", "rRt", "rsg", "gkT", "gqT", "gqeT", "gAiT", "gNm", "gNTm", "invX", "invYT", "invY", "gvb", "gkbg", "gke", "gu", "gwT",
              "gvn", "S", "rthT", "ra1T", "rw2c", "ra2c", "rxs", "rAtT", "rBtT", "rKtT", "rRtT", "rN", "rNT", "rAakT",
              "rArbT", "rArkT", "rX1", "rAt", "rUv", "rWmT", "H0", "H1", "rU", "rBh", "rKh", "rsgTa", "rsgTb",
              "rg2ac", "rg2bc"}
USE_F32R = True


class Buf:
    __slots__ = ("name", "t", "lastw", "readers", "f32r")

    def __init__(self, name, t):
        self.name = name
        self.t = t
        self.lastw = None
        self.readers = []
        self.f32r = USE_F32R and name in F32R_NAMES

    def __getitem__(self, idx):
        return self.t[idx]


class Prog:
    ENG = ("pe", "act", "dve", "pool", "sp")

    def __init__(self, nc):
        self.nc = nc
        self.es = ExitStack()
        self.es_sem = ExitStack()
        self.streams = {e: [] for e in self.ENG}
        self.count = {e: 0 for e in self.ENG}
        self.seen = {e: {} for e in self.ENG}
        self.dma_n = {e: 0 for e in self.ENG}
        self.dma_sems = {}
        self.eng_sems = {}
        self.out_tokens = []
        self.nbuf = 0

    def sb(self, name, shape, dtype=F32):
        self.nbuf += 1
        t = self.es.enter_context(self.nc.sbuf_tensor(f"{name}_{self.nbuf}", list(shape), dtype))
        return Buf(name, t)

    def ps(self, name, shape, dtype=F32):
        self.nbuf += 1
        t = self.es.enter_context(self.nc.psum_tensor(f"{name}_{self.nbuf}", list(shape), dtype))
        return Buf(name, t)

    def tmp(self, name, shape, dtype=F32, n=2):
        if not hasattr(self, "rots"):
            self.rots = {}
        key = (name, tuple(shape), str(dtype))
        if key not in self.rots:
            self.rots[key] = [[self.sb(name, shape, dtype) for _ in range(n)], 0]
        r = self.rots[key]
        b = r[0][r[1] % len(r[0])]
        r[1] += 1
        return b

    def dram(self, name, shape, dtype=F32, kind="Internal"):
        t = self.nc.dram_tensor(name, list(shape), dtype, kind=kind)
        return Buf(name, t)

    def _dsem(self, q, i):
        k = (q, i)
        if k not in self.dma_sems:
            self.dma_sems[k] = self.es_sem.enter_context(self.nc.semaphore(f"d_{q}_{i}"))
        return self.dma_sems[k]

    def _esem(self, e, ep):
        k = (e, ep)
        if k not in self.eng_sems:
            self.eng_sems[k] = self.es_sem.enter_context(self.nc.semaphore(f"e_{e}_{ep}"))
        return self.eng_sems[k]

    def _tok_wait(self, tok):
        if tok[0] == "e":
            _, e, idx = tok
            return (("e", e, idx // EPOCH), idx % EPOCH + 1)
        else:
            _, q, i, val = tok
            return (("d", q, i), val)

    def _deps(self, eng, reads, writes):
        toks = set()
        for b in reads:
            if b.lastw is not None:
                toks.add(b.lastw)
        for b in writes:
            if b.lastw is not None:
                toks.add(b.lastw)
            for r in b.readers:
                toks.add(r)
        waits = []
        seen = self.seen[eng]
        best = {}
        for tok in toks:
            if eng == "pe" and tok[0] == "e" and tok[1] == "pe":
                continue
            k, v = self._tok_wait(tok)
            if seen.get(k, 0) >= v:
                continue
            if best.get(k, 0) < v:
                best[k] = v
        for k, v in best.items():
            seen[k] = v
            waits.append((k, v))
        return waits

    def _commit(self, tok, reads, writes):
        for b in reads:
            b.readers.append(tok)
        for b in writes:
            b.lastw = tok
            b.readers = []

    def capture_begin(self):
        self._cap = []

    def capture_end(self):
        c = self._cap
        self._cap = None
        return c

    def replay(self, ops):
        for o in ops:
            if o[0] == "op":
                self.op(*o[1:])
            else:
                self.dma(o[1], o[2], o[3], reads=o[4], writes=o[5], is_output=o[6])

    def op(self, eng, fn, reads=(), writes=()):
        if getattr(self, "_cap", None) is not None:
            self._cap.append(("op", eng, fn, tuple(reads), tuple(writes)))
            return None
        waits = self._deps(eng, reads, writes)
        idx = self.count[eng]
        self.count[eng] += 1
        tok = ("e", eng, idx)
        self.streams[eng].append(("op", fn, waits, (("e", eng, idx // EPOCH), 1)))
        self._commit(tok, reads, writes)
        return tok

    def dma(self, q, out, in_, reads=(), writes=(), is_output=False):
        if getattr(self, "_cap", None) is not None:
            self._cap.append(("dma", q, out, in_, tuple(reads), tuple(writes), is_output))
            return None
        n = self.dma_n[q]
        self.dma_n[q] += 1
        i = n % NDMASEM
        val = 16 * (n // NDMASEM + 1)
        waits = self._deps(q, reads, writes)
        if val > 16:
            k = ("d", q, i)
            if self.seen[q].get(k, 0) < val - 16:
                self.seen[q][k] = val - 16
                waits.append((k, val - 16))
        tok = ("d", q, i, val)
        self.streams[q].append(("dma", (out, in_), waits, (("d", q, i), 16)))
        self._commit(tok, reads, writes)
        if is_output:
            self.out_tokens.append(tok)
        return tok

    def push_scope(self):
        if not hasattr(self, "rots"):
            self.rots = {}
        self._saved = (self.es, set(self.rots.keys()))
        self.es = ExitStack()

    def pop_scope(self):
        self.barrier()
        self.flush()
        self.es.close()
        self.es, keys = self._saved
        for k in list(self.rots.keys()):
            if k not in keys:
                del self.rots[k]

    def barrier(self):
        lasts = []
        for e in self.ENG:
            if self.count[e] > 0:
                lasts.append(self._tok_wait(("e", e, self.count[e] - 1)))
            n = self.dma_n[e]
            for i in range(min(n, NDMASEM)):
                cnt = (n - 1 - i) // NDMASEM + 1
                lasts.append((("d", e, i), 16 * cnt))
        for e in self.ENG:
            waits = []
            for k, v in lasts:
                if self.seen[e].get(k, 0) < v:
                    self.seen[e][k] = v
                    waits.append((k, v))
            if waits:
                self.streams[e].append(("fin", None, waits, None))

    def flush(self):
        nc = self.nc
        for e in self.ENG:
            for kind, fn, waits, inc in self.streams[e]:
                for k, v in waits:
                    self._sem(k)
                if inc is not None:
                    self._sem(inc[0])
        engmap = {"pe": "tensor", "act": "scalar", "dve": "vector", "pool": "gpsimd", "sp": "sync"}
        with nc.Block() as block:
            for e in self.ENG:
                stream = self.streams[e]
                if not stream:
                    continue

                def body(engobj, stream=stream):
                    pid = None
                    for kind, fn, waits, inc in stream:
                        for k, v in waits:
                            engobj.wait_ge(self._sem(k), v)
                        if kind == "op":
                            ins_ = fn(engobj)
                            ins_.then_inc(self._sem(inc[0]), 1)
                        elif kind == "dma":
                            out, in_ = fn
                            if callable(in_):
                                if pid is None:
                                    pid = engobj.partition_id()
                                in_ = in_(pid)
                            engobj.dma_start(out=out, in_=in_).then_inc(self._sem(inc[0]), 16)
                        elif kind == "idma":
                            out, in_, idx, eoff = fn
                            engobj.indirect_dma_start(out=out, out_offset=None, in_=in_,
                                                      in_offset=bass.IndirectOffsetOnAxis(ap=idx, axis=0),
                                                      element_offset=eoff).then_inc(self._sem(inc[0]), 16)
                        elif kind == "cc":
                            ckind, cin, cout = fn
                            engobj.collective_compute(ckind, ALU.bypass, replica_groups=[list(range(NCORE))],
                                                      ins=[cin], outs=[cout]).then_inc(self._sem(inc[0]))

                getattr(block, engmap[e])(body)
        self.streams = {e: [] for e in self.ENG}

    def collective(self, ckind, cin, cout):
        self.barrier()
        if not hasattr(self, "ncc"):
            self.ncc = 0
        key = ("c", "cc", self.ncc)
        self.ncc += 1
        self.streams["pool"].append(("cc", (ckind, cin, cout), [], (key, 1)))
        for e in self.ENG:
            self.seen[e][key] = 1
            self.streams[e].append(("fin", None, [(key, 1)], None))

    def idma(self, out, in_, idx, eoff=0, reads=(), writes=()):
        q = "pool"
        n = self.dma_n[q]
        self.dma_n[q] += 1
        i = n % NDMASEM
        val = 16 * (n // NDMASEM + 1)
        waits = self._deps(q, reads, writes)
        if val > 16:
            k = ("d", q, i)
            if self.seen[q].get(k, 0) < val - 16:
                self.seen[q][k] = val - 16
                waits.append((k, val - 16))
        tok = ("d", q, i, val)
        self.streams[q].append(("idma", (out, in_, idx, eoff), waits, (("d", q, i), 16)))
        self._commit(tok, reads, writes)
        return tok

    def _sem(self, k):
        if k[0] == "e":
            return self._esem(k[1], k[2])
        if k[0] == "c":
            if k not in self.dma_sems:
                self.dma_sems[k] = self.es_sem.enter_context(self.nc.semaphore(f"cc_{k[2]}"))
            return self.dma_sems[k]
        return self._dsem(k[1], k[2])

    def emit(self):
        nc = self.nc
        best = {}
        for tok in self.out_tokens:
            k, v = self._tok_wait(tok)
            if best.get(k, 0) < v:
                best[k] = v
        fin = list(best.items())
        self.streams["sp"].append(("fin", None, fin, None))
        self.flush()
        self.es.close()
        self.es_sem.close()


D = 2048
KT = 16
NT = 9
NLAT = 8
NTOK = NT * 128
P_IN = 7616
D_FF = 8192
NCORE = 8


class Rot:
    def __init__(self, items):
        self.items = list(items)
        self.i = 0

    def next(self):
        b = self.items[self.i % len(self.items)]
        self.i += 1
        return b


def new_nc():
    return bass.Bass("TRN2", target_bir_lowering=False)


def make_ident(P, n=128):
    io = P.sb("io", [128, n])
    idf = P.sb("idf", [128, n])
    P.op("pool", lambda e: e.iota(io[:], [[1, n]], base=0, channel_multiplier=-1,
                                  allow_small_or_imprecise_dtypes=True), writes=[io])
    P.op("dve", lambda e: e.tensor_single_scalar(out=idf[:], in_=io[:], scalar=0.0, op=ALU.is_equal),
         reads=[io], writes=[idf])
    return idf, io


def bc_feat(vt, j):
    nv = vt.t.shape[1]
    return bass.AP(vt.t, j * KT, [[nv * KT, 128], [1, KT], [0, 128]])


def emit_rstd(P, src, xn, ss, rs):
    P.op("act", lambda e: e.activation(out=xn[:], in_=src[:], func=AF.Square, accum_out=ss[:, 0:1]),
         reads=[src], writes=[xn, ss])
    P.op("dve", lambda e: e.tensor_scalar(out=rs[:], in0=ss[:], scalar1=1.0 / D, scalar2=1e-6,
                                          op0=ALU.mult, op1=ALU.add), reads=[ss], writes=[rs])
    P.op("act", lambda e: e.activation(out=rs[:], in_=rs[:], func=AF.Sqrt), reads=[rs], writes=[rs])
    P.op("dve", lambda e: e.reciprocal(out=rs[:], in_=rs[:]), reads=[rs], writes=[rs])


def emit_norm_T(P, src, dstT, tcol, vt, jg, jb, idf, ptr, xn, tmp, ss, rs, eps_mean=True, bt=None):
    if eps_mean:
        P.op("act", lambda e: e.activation(out=xn[:], in_=src[:], func=AF.Square, accum_out=ss[:, 0:1]),
             reads=[src], writes=[xn, ss])
        P.op("dve", lambda e: e.tensor_scalar(out=rs[:], in0=ss[:], scalar1=1.0 / D, scalar2=1e-6,
                                              op0=ALU.mult, op1=ALU.add), reads=[ss], writes=[rs])
        P.op("act", lambda e: e.activation(out=rs[:], in_=rs[:], func=AF.Sqrt), reads=[rs], writes=[rs])
        P.op("dve", lambda e: e.reciprocal(out=rs[:], in_=rs[:]), reads=[rs], writes=[rs])
        if bt is not None:
            xnb, idb, ptrb = bt
            P.op("dve", lambda e: e.tensor_scalar(out=xnb[:], in0=src[:], scalar1=rs[:, 0:1], scalar2=None,
                                                  op0=ALU.mult), reads=[src, rs], writes=[xnb])
            for k in range(KT):
                P.op("pe", lambda e, k=k: e.transpose(out=ptrb[:, k, :], in_=xnb[:, k * 128:(k + 1) * 128],
                                                      identity=idb[:]), reads=[xnb, idb], writes=[ptrb])
            ptr = ptrb
        else:
            P.op("dve", lambda e: e.tensor_scalar(out=xn[:], in0=src[:], scalar1=rs[:, 0:1], scalar2=None,
                                                  op0=ALU.mult), reads=[src, rs], writes=[xn])
        tsrc = xn
    else:
        tsrc = src
    if bt is None or not eps_mean:
        for k in range(KT):
            P.op("pe", lambda e, k=k: e.transpose(out=ptr[:, k, :], in_=tsrc[:, k * 128:(k + 1) * 128],
                                                  identity=idf[:]), reads=[tsrc, idf], writes=[ptr])
    if jg is None:
        P.op("act", lambda e: e.copy(out=dstT[:, 0:8, tcol:tcol + 128], in_=ptr[:, 0:8, :]),
             reads=[ptr], writes=[dstT])
        P.op("dve", lambda e: e.tensor_copy(out=dstT[:, 8:16, tcol:tcol + 128], in_=ptr[:, 8:16, :]),
             reads=[ptr], writes=[dstT])
    else:
        P.op("dve", lambda e: e.tensor_tensor(out=tmp[:], in0=ptr[:], in1=bc_feat(vt, jg), op=ALU.mult),
             reads=[ptr, vt], writes=[tmp])
        P.op("pool", lambda e: e.tensor_tensor(out=dstT[:, :, tcol:tcol + 128], in0=tmp[:], in1=bc_feat(vt, jb),
                                               op=ALU.add), reads=[tmp, vt], writes=[dstT])


def build_proj():
    nc = new_nc()
    P = Prog(nc)
    xin = P.dram("xin", [NT, 128, D], F32, "ExternalInput")
    vecs = P.dram("vecs", [128, 5, KT], F32, "ExternalInput")
    w = P.dram("w", [D, P_IN], F32, "ExternalInput")
    pout = P.dram("pout", [NT, 128, P_IN], F32, "ExternalOutput")
    idf, _ = make_ident(P)
    vt = P.sb("vt", [128, 5, KT])
    P.dma("sp", vt[:], vecs[:], writes=[vt])
    for j in (1, 3):
        P.op("dve", lambda e, j=j: e.scalar_tensor_tensor(out=vt[:, j, :], in0=vt[:, j, :], scalar=1.0,
                                                          in1=vt[:, 0, :], op0=ALU.add, op1=ALU.mult),
             reads=[vt], writes=[vt])
    hT = P.sb("hT", [128, KT, NTOK], BF16)
    xbufs = Rot([P.sb("x", [128, D]) for _ in range(2)])
    xn = P.sb("xn", [128, D])
    tmp = P.sb("tmp", [128, KT, 128])
    ptr = P.ps("ptr", [128, KT, 128])
    pms = Rot([P.ps("pm", [128, 512]) for _ in range(4)])
    for t in range(NT):
        xs = xbufs.next()
        P.dma("sp", xs[:], xin[t], writes=[xs])
        ss = P.sb("ss", [128, 1])
        rs = P.sb("rs", [128, 1])
        jg, jb = (1, 2) if t < NLAT else (3, 4)
        emit_norm_T(P, xs, hT, t * 128, vt, jg, jb, idf, ptr, xn, tmp, ss, rs)
    slabs = Rot([P.sb("slab", [128, KT, 512], BF16) for _ in range(3)])
    obufs = Rot([P.sb("ob", [128, 512]) for _ in range(4)])
    ncb = (P_IN + 511) // 512
    wv = w.t.ap().rearrange("(k p) n -> p k n", p=128)
    for cb in range(ncb):
        c0 = cb * 512
        ncol = min(512, P_IN - c0)
        slab = slabs.next()
        P.dma("pool", slab[:, :, :ncol], wv[:, :, c0:c0 + ncol], writes=[slab])
        for t in range(NT):
            pm = pms.next()
            for k in range(KT):
                P.op("pe", lambda e, k=k, t=t, pm=pm, slab=slab, ncol=ncol: e.matmul(
                    pm[:, :ncol], lhsT=hT[:, k, t * 128:(t + 1) * 128], rhs=slab[:, k, :ncol],
                    start=(k == 0), stop=(k == KT - 1)), reads=[hT, slab], writes=[pm])
            ob = obufs.next()
            if t % 2 == 0:
                P.op("act", lambda e, pm=pm, ob=ob, ncol=ncol: e.copy(out=ob[:, :ncol], in_=pm[:, :ncol]),
                     reads=[pm], writes=[ob])
            else:
                P.op("dve", lambda e, pm=pm, ob=ob, ncol=ncol: e.tensor_copy(out=ob[:, :ncol], in_=pm[:, :ncol]),
                     reads=[pm], writes=[ob])
            P.dma("sp", pout[t][:, c0:c0 + ncol], ob[:, :ncol], reads=[ob], is_output=True)
    P.emit()
    return nc


def build_ffn(final):
    nc = new_nc()
    P = Prog(nc)
    oin = P.dram("oin", [NT, 128, D], F32, "ExternalInput")
    latin = P.dram("latin", [NT, 128, D], F32, "ExternalInput")
    vecs = P.dram("vecs", [128, 5, KT], F32, "ExternalInput")
    gates = P.dram("gates", [5, D], F32, "ExternalInput")
    wout = P.dram("wout", [D, D], F32, "ExternalInput")
    w1 = P.dram("w1", [D, D_FF], F32, "ExternalInput")
    w2 = P.dram("w2", [D_FF, D], F32, "ExternalInput")
    latout = P.dram("latout", [NT, 128, D], F32, "ExternalOutput")
    idf, _ = make_ident(P)
    vt = P.sb("vt", [128, 5, KT])
    P.dma("sp", vt[:], vecs[:], writes=[vt])
    for j in (1, 3):
        P.op("dve", lambda e, j=j: e.scalar_tensor_tensor(out=vt[:, j, :], in0=vt[:, j, :], scalar=1.0,
                                                          in1=vt[:, 0, :], op0=ALU.add, op1=ALU.mult),
             reads=[vt], writes=[vt])
    GT = P.sb("GT", [128, 2, D])

    def load_gates(j0, n):
        P.dma("sp", GT[:, 0:n, :], bass.AP(gates.t, j0 * D, [[0, 128], [D, n], [1, D]]), writes=[GT])

    load_gates(0, 2)
    acc = [P.sb("acc", [128, D]) for _ in range(NT)]
    for t in range(NT):
        P.dma("sp", acc[t][:], latin[t], writes=[acc[t]])
    hT = P.sb("hT", [128, KT, NTOK], BF16)
    xn = P.sb("xn", [128, D])
    otile = xn
    tmp = P.sb("tmp", [128, KT, 128])
    ptr = P.ps("ptr", [128, KT, 128])
    pms = Rot([P.ps("pm", [128, 512]) for _ in range(4)])
    tmp2s = Rot([P.sb("tmp2", [128, 512]) for _ in range(3)])
    slabs = Rot([P.sb("slab", [128, KT, 512], BF16) for _ in range(2)])
    zTs = Rot([P.sb("zT", [128, 4, NTOK], BF16) for _ in range(2)])
    for t in range(NT):
        P.dma("sp", otile[:], oin[t], writes=[otile])
        emit_norm_T(P, otile, hT, t * 128, vt, None, None, idf, ptr, xn, tmp, None, None, eps_mean=False)

    def gated_acc(pm, t, c0, gslot):
        tmp2 = tmp2s.next()
        P.op("dve", lambda e: e.tensor_tensor(out=tmp2[:], in0=pm[:], in1=GT[:, gslot, c0:c0 + 512], op=ALU.mult),
             reads=[pm, GT], writes=[tmp2])
        P.op("pool", lambda e: e.tensor_tensor(out=acc[t][:, c0:c0 + 512], in0=acc[t][:, c0:c0 + 512], in1=tmp2[:],
                                               op=ALU.add), reads=[acc[t], tmp2], writes=[acc[t]])

    wv = wout.t.ap().rearrange("(k p) n -> p k n", p=128)
    for cb in range(4):
        c0 = cb * 512
        slab = slabs.next()
        P.dma("pool", slab[:], wv[:, :, c0:c0 + 512], writes=[slab])
        for t in range(NT):
            pm = pms.next()
            for k in range(KT):
                P.op("pe", lambda e, k=k, t=t, pm=pm, slab=slab: e.matmul(
                    pm[:], lhsT=hT[:, k, t * 128:(t + 1) * 128], rhs=slab[:, k, :],
                    start=(k == 0), stop=(k == KT - 1)), reads=[hT, slab], writes=[pm])
            gated_acc(pm, t, c0, 0 if t < NLAT else 1)
    for t in range(NT):
        ss = P.sb("ss", [128, 1])
        rs = P.sb("rs", [128, 1])
        jg, jb = (1, 2) if t < NLAT else (3, 4)
        emit_norm_T(P, acc[t], hT, t * 128, vt, jg, jb, idf, ptr, xn, tmp, ss, rs)
    load_gates(2, 2)
    w1v = w1.t.ap().rearrange("(k p) n -> p k n", p=128)
    groups = [(0, 512), (512, 512), (1024, 128)]
    rrs = Rot([P.sb("rr", [128, 512]) for _ in range(2)])
    for c in range(D_FF // 512):
        slabA = slabs.next()
        P.dma("pool", slabA[:], w1v[:, :, c * 512:(c + 1) * 512], writes=[slabA])
        zT = zTs.next()
        for f in range(4):
            for (g0, gn) in groups:
                pm = pms.next()
                for k in range(KT):
                    P.op("pe", lambda e, k=k, f=f, pm=pm, slabA=slabA, g0=g0, gn=gn: e.matmul(
                        pm[:, :gn], lhsT=slabA[:, k, f * 128:(f + 1) * 128], rhs=hT[:, k, g0:g0 + gn],
                        start=(k == 0), stop=(k == KT - 1)), reads=[hT, slabA], writes=[pm])
                rr = rrs.next()
                P.op("act", lambda e, pm=pm, rr=rr, gn=gn: e.activation(out=rr[:, :gn], in_=pm[:, :gn], func=AF.Relu),
                     reads=[pm], writes=[rr])
                P.op("dve", lambda e, rr=rr, zT=zT, f=f, g0=g0, gn=gn: e.tensor_tensor(
                    out=zT[:, f, g0:g0 + gn], in0=rr[:, :gn], in1=rr[:, :gn], op=ALU.mult), reads=[rr], writes=[zT])
        slabB = slabs.next()
        P.dma("pool", slabB.t.ap().rearrange("p (f b) n -> p f (b n)", f=4), w2.t.ap()[c * 512:(c + 1) * 512, :].rearrange("(f p) n -> p f n", p=128),
              writes=[slabB])
        for t in range(NT):
            for b in range(4):
                pm = pms.next()
                for f in range(4):
                    P.op("pe", lambda e, f=f, t=t, b=b, pm=pm, slabB=slabB, zT=zT: e.matmul(
                        pm[:], lhsT=zT[:, f, t * 128:(t + 1) * 128], rhs=slabB[:, f * 4 + b, :],
                        start=(f == 0), stop=(f == 3)), reads=[zT, slabB], writes=[pm])
                gated_acc(pm, t, b * 512, 0 if t < NLAT else 1)
    if final:
        load_gates(4, 1)
    for t in range(NT):
        if final:
            ss = P.sb("ss", [128, 1])
            rs = P.sb("rs", [128, 1])
            emit_rstd(P, acc[t], xn, ss, rs)
            P.op("dve", lambda e, t=t, rs=rs: e.scalar_tensor_tensor(out=acc[t][:], in0=acc[t][:], scalar=rs[:, 0:1],
                                                                     in1=GT[:, 0, :], op0=ALU.mult, op1=ALU.mult),
                 reads=[acc[t], rs, GT], writes=[acc[t]])
        P.dma("sp", latout[t], acc[t][:], reads=[acc[t]], is_output=True)
    P.emit()
    return nc


MODC = 6 * D // NCORE


def build_mod():
    nc = new_nc()
    P = Prog(nc)
    cT = P.dram("cT", [128, KT, 2], F32, "ExternalInput")
    aw = P.dram("aw", [2, D, MODC], F32, "ExternalInput")
    ab = P.dram("ab", [2, MODC], F32, "ExternalInput")
    mod = P.dram("mod", [2, 2, MODC], F32, "ExternalOutput")
    cs = P.sb("cs", [128, KT, 2])
    s = P.sb("s", [128, KT, 2])
    P.dma("sp", cs[:], cT[:], writes=[cs])
    P.op("act", lambda e: e.activation(out=s[:], in_=cs[:], func=AF.Silu), reads=[cs], writes=[s])
    bia = P.sb("bia", [2, 2, MODC])
    P.dma("sp", bia[:], bass.AP(ab.t, 0, [[0, 2], [MODC, 2], [1, MODC]]), writes=[bia])
    slabs = Rot([P.sb("slab", [128, KT, 512]) for _ in range(2)])
    pms = Rot([P.ps("pm", [128, 512]) for _ in range(2)])
    obs = Rot([P.sb("ob", [2, 512]) for _ in range(2)])
    for l in range(2):
        wv = aw.t.ap()[l].rearrange("(k p) n -> p k n", p=128)
        for b in range(MODC // 512):
            slab = slabs.next()
            for k in range(KT):
                P.dma("sp" if k % 2 == 0 else "act", slab[:, k, :], wv[:, k, b * 512:(b + 1) * 512], writes=[slab])
            pm = pms.next()
            for k in range(KT):
                P.op("pe", lambda e, k=k, pm=pm, slab=slab: e.matmul(pm[0:2, :], lhsT=s[:, k, :], rhs=slab[:, k, :],
                                                                     start=(k == 0), stop=(k == KT - 1)),
                     reads=[s, slab], writes=[pm])
            ob = obs.next()
            P.op("dve", lambda e, pm=pm, ob=ob, l=l, b=b: e.tensor_tensor(
                out=ob[:], in0=pm[0:2, :], in1=bia[:, l, b * 512:(b + 1) * 512], op=ALU.add),
                 reads=[pm, bia], writes=[ob])
            P.dma("sp", mod[l][:, b * 512:(b + 1) * 512], ob[:], reads=[ob], is_output=True)
    P.emit()
    return nc


class _VIdx:
    def __init__(self, buf):
        self.buf = buf

    def __getitem__(self, idx):
        return (self.buf, self.buf.t[idx])


def _V(self):
    return _VIdx(self)


Buf.V = property(_V)


def ins(v, axis, n):
    buf, ap = v
    l = [list(x) for x in ap.ap]
    l.insert(axis, [0, n])
    return (buf, bass.AP(ap.tensor, ap.offset, l))


def rv(v, pattern, **kw):
    return (v[0], v[1].rearrange(pattern, **kw))


def _o(v):
    return v[1].bitcast(F32R) if v[0].f32r else v[1]


class SB:
    def __init__(self, P):
        self.P = P

    def tt(self, eng, out, a, b, op):
        self.P.op(eng, lambda e: e.tensor_tensor(out=_o(out), in0=a[1], in1=b[1], op=op),
                  reads=[a[0], b[0]], writes=[out[0]])

    def ts(self, eng, out, a, s1, op0, s2=None, op1=None):
        reads = [a[0]]
        s1v = s1
        s2v = s2
        if isinstance(s1, tuple):
            reads.append(s1[0]); s1v = s1[1]
        if isinstance(s2, tuple):
            reads.append(s2[0]); s2v = s2[1]
        if op1 is None:
            self.P.op(eng, lambda e: e.tensor_scalar(out=_o(out), in0=a[1], scalar1=s1v, scalar2=None, op0=op0),
                      reads=reads, writes=[out[0]])
        else:
            self.P.op(eng, lambda e: e.tensor_scalar(out=_o(out), in0=a[1], scalar1=s1v, scalar2=s2v, op0=op0, op1=op1),
                      reads=reads, writes=[out[0]])

    def stt(self, eng, out, a, scalar, b, op0, op1):
        reads = [a[0], b[0]]
        sv = scalar
        if isinstance(scalar, tuple):
            reads.append(scalar[0]); sv = scalar[1]
        self.P.op(eng, lambda e: e.scalar_tensor_tensor(out=_o(out), in0=a[1], scalar=sv, in1=b[1], op0=op0, op1=op1),
                  reads=reads, writes=[out[0]])

    def act(self, out, a, func, scale=1.0, bias=0.0):
        reads = [a[0]]
        bv = bias
        if isinstance(bias, tuple):
            reads.append(bias[0]); bv = bias[1]
        self.P.op("act", lambda e: e.activation(out=_o(out), in_=a[1], func=func, bias=bv, scale=scale),
                  reads=reads, writes=[out[0]])

    def cp(self, eng, out, a):
        if eng == "act":
            self.P.op("act", lambda e: e.copy(out=_o(out), in_=a[1]), reads=[a[0]], writes=[out[0]])
        else:
            self.P.op(eng, lambda e: e.tensor_copy(out=_o(out), in_=a[1]), reads=[a[0]], writes=[out[0]])

    def mm(self, out, lhsT, rhs, start=True, stop=True):
        la, ra = lhsT[1], rhs[1]
        if lhsT[0].f32r and rhs[0].f32r:
            la, ra = la.bitcast(F32R), ra.bitcast(F32R)
        self.P.op("pe", lambda e: e.matmul(out[1], lhsT=la, rhs=ra, start=start, stop=stop),
                  reads=[lhsT[0], rhs[0]], writes=[out[0]])

    def tr(self, out, a, ident):
        oa, ia, da = out[1], a[1], ident[1]
        reads = [a[0], ident[0]]
        if a[0].f32r and getattr(self, "identr", None) is not None:
            n_ = ident[1].shape[0]
            oa, ia, da = oa.bitcast(F32R), ia.bitcast(F32R), self.identr.t[0:n_, 0:n_].bitcast(F32R)
            reads = [a[0], self.identr]
        self.P.op("pe", lambda e: e.transpose(out=oa, in_=ia, identity=da), reads=reads, writes=[out[0]])

    def red(self, eng, out, a):
        self.P.op(eng, lambda e: e.tensor_reduce(out=_o(out), in_=a[1], axis=AX.X, op=ALU.add),
                  reads=[a[0]], writes=[out[0]])

    def memset(self, eng, out, val):
        if out[0].f32r:
            z = self.P.tmp("zeros_c", [128, 128], n=1)
            self.P.op(eng, lambda e: e.memset(z[:], val), writes=[z])
            np_, nf = out[1].shape[0], out[1].shape[1]
            self.cp(eng, out, (z, z.t[0:np_, 0:nf]))
            return
        self.P.op(eng, lambda e: e.memset(out[1], val), writes=[out[0]])


C = 64
NB = 4


class ScanConsts:
    def __init__(self, P):
        s = SB(P)
        io = P.sb("io", [128, 128])
        P.op("pool", lambda e: e.iota(io[:], [[1, 128]], base=0, channel_multiplier=-1,
                                      allow_small_or_imprecise_dtypes=True), writes=[io])
        self.ident = P.sb("ident", [128, 128])
        s.ts("dve", self.ident.V[:], io.V[:], 0.0, ALU.is_equal)
        self.minc = [P.sb("minc0", [C, C]), P.sb("minc1", [C, C])]
        self.mstr = [P.sb("mstr0", [C, C]), P.sb("mstr1", [C, C])]
        s.ts("dve", self.minc[0].V[:], io.V[0:C, 0:C], 0.0, ALU.is_ge)
        s.ts("dve", self.mstr[0].V[:], io.V[0:C, 0:C], 0.0, ALU.is_gt)
        s.ts("dve", self.minc[1].V[:], io.V[0:C, 0:C], 0.0, ALU.is_le)
        s.ts("dve", self.mstr[1].V[:], io.V[0:C, 0:C], 0.0, ALU.is_lt)
        self.ones = P.sb("ones", [C, 128])
        s.memset("dve", self.ones.V[:], 1.0)
        self.ps = Rot([P.ps("ps", [128, 512]) for _ in range(5)])
        self.ps2 = Rot([P.ps("ps2", [128, 512]) for _ in range(3)])


def bcast_free(s, K, vec, m):
    P = s.P
    dg = P.tmp("dg", [C, m, C])
    s.tt("dve", dg.V[:], ins(K.ident.V[0:C, 0:C], 1, m), ins(vec, 2, C), ALU.mult)
    pm = K.ps.next()
    s.mm(pm.V[0:C, 0:m * C], K.ones.V[:, 0:C], rv(dg.V[:], "p n i -> p (n i)"))
    return rv(pm.V[0:C, 0:m * C], "p (n i) -> p n i", i=C)


def inverse_doubling(s, K, N, NTm, m):
    P = s.P
    X = P.tmp("invX", [C, m, C], n=3)
    s.tt("dve", X.V[:], N.V[:], ins(K.ident.V[0:C, 0:C], 1, m), ALU.add)
    Y, YT = N, NTm
    for lvl in range(5):
        pmT = K.ps.next()
        for j in range(m):
            s.mm(pmT.V[0:C, j * C:(j + 1) * C], Y.V[:, j, :], YT.V[:, j, :])
        YT2 = P.tmp("invYT", [C, m, C])
        s.cp("act", rv(YT2.V[:], "p n i -> p (n i)"), pmT.V[0:C, 0:m * C])
        if lvl < 4:
            pmY = K.ps.next()
            for j in range(m):
                s.mm(pmY.V[0:C, j * C:(j + 1) * C], YT.V[:, j, :], Y.V[:, j, :])
            Y2 = P.tmp("invY", [C, m, C])
            s.cp("dve", rv(Y2.V[:], "p n i -> p (n i)"), pmY.V[0:C, 0:m * C])
        pmX = K.ps.next()
        for j in range(m):
            s.mm(pmX.V[0:C, j * C:(j + 1) * C], YT2.V[:, j, :], X.V[:, j, :])
        Xn = P.tmp("invX", [C, m, C], n=3)
        s.tt("dve", rv(Xn.V[:], "p n i -> p (n i)"), rv(X.V[:], "p n i -> p (n i)"), pmX.V[0:C, 0:m * C], ALU.add)
        X = Xn
        YT = YT2
        if lvl < 4:
            Y = Y2
    return X


def merge_ops(a, b):
    out = []
    ia = ib = 0
    la, lb = len(a), len(b)
    while ia < la or ib < lb:
        if ib >= lb or (ia < la and ia * lb <= ib * la):
            out.append(a[ia]); ia += 1
        else:
            out.append(b[ib]); ib += 1
    return out


def pipeline_blocks(P, seq, prep, steps):
    P.capture_begin()
    ctx = prep(*seq[0])
    P.replay(P.capture_end())
    for i in range(len(seq)):
        P.capture_begin()
        steps(ctx)
        s_ops = P.capture_end()
        p_ops = []
        if i + 1 < len(seq):
            P.capture_begin()
            ctx = prep(*seq[i + 1])
            p_ops = P.capture_end()
        P.replay(merge_ops(s_ops, p_ops))


GW = 516
NCH = 132
NBLK = NCH // NB
GDN_ROWS = 8452


def gdn_block_row0(b):
    return 1 if b == 0 else 259 + (b - 1) * NB * C


def emit_gdn(P, K, Pg, gcw, gsc, gng, og, ocol=0, tag="", is_out=True):
    s = SB(P)
    A = ALU
    identr = P.sb("identr", [128, 128])
    s.cp("dve", identr.V[:], K.ident.V[:])
    s.identr = identr
    cw = P.sb("cw", [C, 3, 384])
    P.dma("sp", cw[:], bass.AP(gcw.t, 0, [[0, C], [384, 3], [1, 384]]), writes=[cw])
    sc = P.sb("gsc", [C, 4])
    P.dma("sp", sc[:], bass.AP(gsc.t, 0, [[0, C], [1, 4]]), writes=[sc])
    gn = P.sb("gn", [C, 128])
    P.dma("sp", gn[:], bass.AP(gng.t, 0, [[0, C], [1, 128]]), writes=[gn])
    negA = P.sb("negA", [C, 2])
    s.act(negA.V[:], sc.V[:, 0:2], AF.Exp)
    s.ts("dve", negA.V[:], negA.V[:], -1.0, A.mult)
    S = P.sb("S", [128, 128])
    ofs = P.dram("gdn_ofs" + tag, [NCH * C, 128], F32)
    def prep(d, b, first):
        r0 = gdn_block_row0(b)
        cur = P.tmp("gcur", [C, NB, GW], n=2)
        prv = P.tmp("gprv", [C, NB, 384], n=2)
        nxt = P.tmp("gnxt", [C, NB, 384], n=2)
        pga = Pg.t.ap()
        P.dma("sp", cur[:], pga[r0:r0 + NB * C, :].rearrange("(n j) c -> j n c", j=C), reads=[Pg], writes=[cur])
        P.dma("act", prv[:], pga[r0 - 1:r0 - 1 + NB * C, 0:384].rearrange("(n j) c -> j n c", j=C),
              reads=[Pg], writes=[prv])
        P.dma("sp", nxt[:], pga[r0 + 1:r0 + 1 + NB * C, 0:384].rearrange("(n j) c -> j n c", j=C),
              reads=[Pg], writes=[nxt])
        qkv = P.tmp("gqkv", [C, NB, 384], n=2)
        t1 = P.tmp("gt1", [C, NB, 384], n=2)
        s.tt("dve", qkv.V[:], cur.V[:, :, 0:384], ins(cw.V[:, 1, :], 1, NB), A.mult)
        s.tt("pool", t1.V[:], prv.V[:], ins(cw.V[:, 0, :], 1, NB), A.mult)
        s.tt("dve", qkv.V[:], qkv.V[:], t1.V[:], A.add)
        t2 = P.tmp("gt1", [C, NB, 384], n=2)
        s.tt("pool", t2.V[:], nxt.V[:], ins(cw.V[:, 2, :], 1, NB), A.mult)
        s.tt("dve", qkv.V[:], qkv.V[:], t2.V[:], A.add)
        s.act(qkv.V[:], qkv.V[:], AF.Silu)
        sq = P.tmp("gsq", [C, NB, 256], n=1)
        s.tt("pool", sq.V[:], qkv.V[:, :, 0:256], qkv.V[:, :, 0:256], A.mult)
        ssq = P.tmp("gssq", [C, NB, 2], n=2)
        s.red("dve", ssq.V[:], rv(sq.V[:], "p n (h f) -> p n h f", h=2))
        s.ts("dve", ssq.V[:], ssq.V[:], 1e-6, A.add)
        s.act(ssq.V[:], ssq.V[:], AF.Sqrt)
        rs = P.tmp("grs", [C, NB, 2], n=2)
        P.op("dve", lambda e, rs=rs, ssq=ssq: e.reciprocal(out=rs[:], in_=ssq[:]), reads=[ssq], writes=[rs])
        s.ts("dve", rs.V[:, :, 0:1], rs.V[:, :, 0:1], float(128 ** -0.5), A.mult)
        qk = P.tmp("gqk", [C, NB, 256], n=2)
        s.tt("dve", rv(qk.V[:], "p n (h f) -> p n h f", h=2), rv(qkv.V[:, :, 0:256], "p n (h f) -> p n h f", h=2),
             ins(rs.V[:], 3, 128), A.mult)
        be = P.tmp("gbe", [C, NB, 2], n=2)
        s.act(be.V[:], cur.V[:, :, 512:514], AF.Sigmoid)
        xx = P.tmp("gxx", [C, NB, 2], n=2)
        s.tt("dve", xx.V[:], cur.V[:, :, 514:516], ins(sc.V[:, 2:4], 1, NB), A.add)
        s.act(xx.V[:], xx.V[:], AF.Exp)
        s.act(xx.V[:], xx.V[:], AF.Ln, bias=1.0)
        ld = P.tmp("gld", [C, NB, 2], n=2)
        s.tt("dve", ld.V[:], xx.V[:], ins(negA.V[:], 1, NB), A.mult)
        ldd = ld.V[:, :, d]
        pm = K.ps.next()
        s.mm(pm.V[0:C, 0:NB], K.minc[d].V[:], ldd)
        g = P.tmp("gg", [C, NB], n=2)
        s.cp("act", g.V[:], pm.V[0:C, 0:NB])
        pm = K.ps.next()
        s.mm(pm.V[0:128, 0:NB], K.ones.V[:, 0:128], ldd)
        dl = P.tmp("gdl", [128, NB], n=2)
        s.act(dl.V[:], pm.V[0:128, 0:NB], AF.Exp)
        egl = P.tmp("gegl", [C, NB], n=2)
        s.tt("dve", egl.V[:], pm.V[0:C, 0:NB], g.V[:], A.subtract)
        s.act(egl.V[:], egl.V[:], AF.Exp)
        eg = P.tmp("geg", [C, NB], n=2)
        s.act(eg.V[:], g.V[:], AF.Exp)
        GI = bcast_free(s, K, g.V[:], NB)
        Dm = P.tmp("gDm", [C, NB, C], n=2)
        s.tt("dve", Dm.V[:], GI, ins(g.V[:], 2, C), A.subtract)
        Dn = P.tmp("gDn", [C, NB, C], n=2)
        s.ts("dve", Dn.V[:], Dm.V[:], 0.0, A.min)
        s.ts("pool", Dm.V[:], Dm.V[:], 0.0, A.max)
        s.act(Dn.V[:], Dn.V[:], AF.Exp)
        s.act(Dm.V[:], Dm.V[:], AF.Exp, scale=-1.0)
        DecI = P.tmp("gDecI", [C, NB, C], n=2)
        s.tt("pool", DecI.V[:], Dn.V[:], ins(K.minc[d].V[:], 1, NB), A.mult)
        nbe = P.tmp("gnbe", [C, NB], n=2)
        s.ts("dve", nbe.V[:], be.V[:, :, d], -1.0, A.mult)
        BI = bcast_free(s, K, nbe.V[:], NB)
        W1 = P.tmp("gW1", [C, NB, C], n=2)
        s.tt("pool", W1.V[:], Dn.V[:], ins(K.mstr[d].V[:], 1, NB), A.mult)
        s.tt("dve", W1.V[:], W1.V[:], BI, A.mult)
        W2 = P.tmp("gW2", [C, NB, C], n=2)
        s.tt("pool", W2.V[:], Dm.V[:], ins(K.mstr[1 - d].V[:], 1, NB), A.mult)
        s.tt("pool", W2.V[:], W2.V[:], ins(nbe.V[:], 2, C), A.mult)
        qe = P.tmp("gqe", [C, NB, 128], n=2)
        s.tt("pool", qe.V[:], qk.V[:, :, 0:128], ins(eg.V[:], 2, 128), A.mult)
        kT = P.tmp("gkT", [128, NB, C], n=2)
        qT = P.tmp("gqT", [128, NB, C], n=2)
        qeT = P.tmp("gqeT", [128, NB, C], n=2)
        for (dst, src, eng) in ((kT, qk.V[:, :, 128:256], "act"), (qT, qk.V[:, :, 0:128], "dve"), (qeT, qe.V[:], "act")):
            pm = K.ps.next()
            for n in range(NB):
                s.tr(pm.V[0:128, n * C:(n + 1) * C], (src[0], src[1][:, n, :]), K.ident.V[0:C, 0:C])
            s.cp(eng, rv(dst.V[:], "p n i -> p (n i)"), pm.V[0:128, 0:NB * C])
        pmK = K.ps.next()
        pmA = K.ps.next()
        for n in range(NB):
            s.mm(pmK.V[0:C, n * C:(n + 1) * C], kT.V[:, n, :], kT.V[:, n, :])
            s.mm(pmA.V[0:C, n * C:(n + 1) * C], kT.V[:, n, :], qT.V[:, n, :])
        AiT = P.tmp("gAiT", [C, NB, C], n=2)
        s.tt("dve", rv(AiT.V[:], "p n i -> p (n i)"), pmA.V[0:C, 0:NB * C], rv(DecI.V[:], "p n i -> p (n i)"), A.mult)
        Nm = P.tmp("gNm", [C, NB, C], n=2)
        NTm = P.tmp("gNTm", [C, NB, C], n=2)
        s.tt("dve", rv(Nm.V[:], "p n i -> p (n i)"), pmK.V[0:C, 0:NB * C], rv(W1.V[:], "p n i -> p (n i)"), A.mult)
        s.tt("dve", rv(NTm.V[:], "p n i -> p (n i)"), pmK.V[0:C, 0:NB * C], rv(W2.V[:], "p n i -> p (n i)"), A.mult)
        TT = inverse_doubling(s, K, Nm, NTm, NB)
        vb = P.tmp("gvb", [C, NB, 128], n=2)
        s.tt("pool", vb.V[:], qkv.V[:, :, 256:384], ins(be.V[:, :, d], 2, 128), A.mult)
        beg = P.tmp("gbeg", [C, NB], n=2)
        s.tt("dve", beg.V[:], be.V[:, :, d], eg.V[:], A.mult)
        kbg = P.tmp("gkbg", [C, NB, 128], n=2)
        s.tt("pool", kbg.V[:], qk.V[:, :, 128:256], ins(beg.V[:], 2, 128), A.mult)
        ke = P.tmp("gke", [C, NB, 128], n=2)
        s.tt("pool", ke.V[:], qk.V[:, :, 128:256], ins(egl.V[:], 2, 128), A.mult)
        pmU = K.ps.next()
        pmW = K.ps.next()
        for n in range(NB):
            s.mm(pmU.V[0:C, n * 128:(n + 1) * 128], TT.V[:, n, :], vb.V[:, n, :])
            s.mm(pmW.V[0:128, n * C:(n + 1) * C], kbg.V[:, n, :], TT.V[:, n, :])
        u = P.tmp("gu", [C, NB, 128], n=2)
        wT = P.tmp("gwT", [128, NB, C], n=2)
        s.cp("act", rv(u.V[:], "p n i -> p (n i)"), pmU.V[0:C, 0:NB * 128])
        s.cp("dve", rv(wT.V[:], "p n i -> p (n i)"), pmW.V[0:128, 0:NB * C])
        return dict(d=d, b=b, first=first, wT=wT, u=u, qeT=qeT, AiT=AiT, ke=ke, dl=dl, cur=cur)

    def steps(c_):
        d, b = c_["d"], c_["b"]
        wT = c_["wT"]
        u = c_["u"]
        qeT = c_["qeT"]
        AiT = c_["AiT"]
        ke = c_["ke"]
        dl = c_["dl"]
        cur = c_["cur"]
        if c_["first"]:
            s.memset("dve", S.V[:], 0.0)
        O = P.tmp("gO", [C, NB, 128], n=2)
        t0 = b * NB * C
        ofv = ofs.t.ap()[t0:t0 + NB * C, :].rearrange("(n j) c -> j n c", j=C)
        if d == 1:
            P.dma("act", O[:], ofv, reads=[ofs], writes=[O])
        for n in (range(NB) if d == 0 else range(NB - 1, -1, -1)):
            pm1 = K.ps2.next()
            s.mm(pm1.V[0:C, 0:128], wT.V[:, n, :], S.V[:])
            vn = P.tmp("gvn", [C, 128], n=3)
            s.tt("dve", vn.V[:], u.V[:, n, :], pm1.V[0:C, 0:128], A.subtract)
            pm2 = K.ps2.next()
            s.mm(pm2.V[0:C, 0:128], qeT.V[:, n, :], S.V[:], start=True, stop=False)
            s.mm(pm2.V[0:C, 0:128], AiT.V[:, n, :], vn.V[:], start=False, stop=True)
            if d == 0:
                s.cp("act", O.V[:, n, :], pm2.V[0:C, 0:128])
            else:
                s.tt("pool" if False else "dve", O.V[:, n, :], O.V[:, n, :], pm2.V[0:C, 0:128], A.add)
            pm3 = K.ps2.next()
            s.mm(pm3.V[0:128, 0:128], ke.V[:, n, :], vn.V[:])
            s.stt("dve", S.V[:], S.V[:], dl.V[:, n:n + 1], pm3.V[0:128, 0:128], A.mult, A.add)
        if d == 0:
            P.dma("act", ofv, O[:], reads=[O], writes=[ofs])
        if d == 1:
            sq2 = P.tmp("gsq2", [C, NB, 128], n=1)
            s.tt("pool", sq2.V[:], O.V[:], O.V[:], A.mult)
            ms = P.tmp("gms", [C, NB], n=2)
            s.red("dve", ms.V[:], sq2.V[:])
            s.ts("dve", ms.V[:], ms.V[:], 1.0 / 128, A.mult, 1e-6, A.add)
            s.act(ms.V[:], ms.V[:], AF.Sqrt)
            rr = P.tmp("grr", [C, NB], n=2)
            P.op("dve", lambda e, rr=rr, ms=ms: e.reciprocal(out=rr[:], in_=ms[:]), reads=[ms], writes=[rr])
            y = P.tmp("gy", [C, NB, 128], n=2)
            s.tt("dve", y.V[:], O.V[:], ins(rr.V[:], 2, 128), A.mult)
            s.tt("pool", y.V[:], y.V[:], ins(gn.V[:], 1, NB), A.mult)
            zs = P.tmp("gzs", [C, NB, 128], n=2)
            s.act(zs.V[:], cur.V[:, :, 384:512], AF.Silu)
            s.tt("dve", y.V[:], y.V[:], zs.V[:], A.mult)
            P.dma("sp", og.t.ap()[t0:t0 + NB * C, ocol:ocol + 128].rearrange("(n j) c -> j n c", j=C), y[:], reads=[y],
                  is_output=is_out)

    seq = []
    for d in range(2):
        order = list(range(NBLK)) if d == 0 else [0] + list(range(NBLK - 1, 0, -1))
        for j_, b in enumerate(order):
            seq.append((d, b, j_ == 0))
    pipeline_blocks(P, seq, prep, steps)


def build_gdn_test():
    nc = new_nc()
    P = Prog(nc)
    Pg = P.dram("Pg", [GDN_ROWS, GW], F32, "ExternalInput")
    gcw = P.dram("gcw", [3, 384], F32, "ExternalInput")
    gsc = P.dram("gsc", [4], F32, "ExternalInput")
    gng = P.dram("gng", [128], F32, "ExternalInput")
    og = P.dram("og", [NCH * C, 128], F32, "ExternalOutput")
    K = ScanConsts(P)
    emit_gdn(P, K, Pg, gcw, gsc, gng, og)
    P.emit()
    return nc


P_GDN = 4128


def gdn_cols(i):
    h = np.arange(128)
    return np.concatenate([i * 128 + h, 1024 + i * 128 + h, 2048 + i * 128 + h, 3072 + i * 128 + h,
                           [4096 + i, 4104 + i, 4112 + i, 4120 + i]]).astype(np.int64)


def gdn_conv_cols(i):
    h = np.arange(128)
    return np.concatenate([i * 128 + h, 1024 + i * 128 + h, 2048 + i * 128 + h]).astype(np.int64)


RW = 800
RW_ROWS = 8577
RW_LAT0 = 321
NM = 2 * NB


def emit_rwkv(P, K, Pr, rmu, rvec, rw2, ra2, rg2, orr, ocol=0, tag="", is_out=True):
    s = SB(P)
    A = ALU
    identr = P.sb("identr", [128, 128])
    s.cp("dve", identr.V[:], K.ident.V[:])
    s.identr = identr
    mu = P.sb("rmu", [C, RW])
    P.dma("sp", mu[:], bass.AP(rmu.t, 0, [[0, C], [1, RW]]), writes=[mu])
    vec = P.sb("rvec", [C, 9, 128])
    P.dma("sp", vec[:], bass.AP(rvec.t, 0, [[0, C], [128, 9], [1, 128]]), writes=[vec])
    w2 = P.sb("rw2", [C, 2, 128])
    a2 = P.sb("ra2", [C, 2, 128])
    P.dma("sp", w2[:], rw2.t.ap().rearrange("d r c -> r d c"), writes=[w2])
    P.dma("sp", a2[:], ra2.t.ap().rearrange("d r c -> r d c"), writes=[a2])
    g2a = P.sb("rg2a", [128, 128])
    g2b = P.sb("rg2b", [32, 128])
    P.dma("sp", g2a[:], rg2.t.ap()[0:128, :], writes=[g2a])
    P.dma("sp", g2b[:], rg2.t.ap()[128:160, :], writes=[g2b])
    w2_, a2_, g2a_, g2b_ = w2, a2, g2a, g2b
    w2 = P.sb("rw2c", [C, 2, 128])
    a2 = P.sb("ra2c", [C, 2, 128])
    g2a = P.sb("rg2ac", [128, 128])
    g2b = P.sb("rg2bc", [32, 128])
    s.cp("act", w2.V[:], w2_.V[:])
    s.cp("act", a2.V[:], a2_.V[:])
    s.cp("act", g2a.V[:], g2a_.V[:])
    s.cp("act", g2b.V[:], g2b_.V[:])
    ofs = P.dram("rw_ofs" + tag, [NCH * C, 130], F32)
    H = [P.sb("H0", [C, C]), P.sb("H1", [C, C])]
    W = RW
    def prep(d, b, first):
        cur = P.tmp("rcur", [C, NB, W], n=1)
        prv = P.tmp("rprv", [C, NB, W], n=1)
        nxt = P.tmp("rnxt", [C, NB, W], n=1)
        if b == 0:
            pra = Pr.t.ap()
            for (dst, off, q) in ((cur, 0, "sp"), (prv, -1, "act"), (nxt, 1, "sp")):
                P.dma(q, dst[:], pra[1 + off:1 + off + NB * C, :].rearrange("(n j) c -> j n c", j=C),
                      reads=[Pr], writes=[dst])
        else:
            c0 = (b - 1) * 2
            for (dst, off, q) in ((cur, 0, "sp"), (prv, -64, "act"), (nxt, 64, "sp")):
                for cc in range(2):
                    src = bass.AP(Pr.t, (RW_LAT0 + off + c0 + cc) * W, [[64 * W, C], [64 * 64 * W, 2], [1, W]])
                    P.dma(q, dst[:, 2 * cc:2 * cc + 2, :], src, reads=[Pr], writes=[dst])
        xs = P.tmp("rxs", [C, NB, W], n=2)
        s.tt("dve", xs.V[:], prv.V[:], nxt.V[:], A.add)
        s.stt("dve", xs.V[:], xs.V[:], 0.5, cur.V[:], A.mult, A.subtract)
        s.tt("dve", xs.V[:], xs.V[:], ins(mu.V[:], 1, NB), A.mult)
        s.tt("dve", xs.V[:], xs.V[:], cur.V[:], A.add)
        r_ = xs.V[:, :, 0:128]
        k_ = xs.V[:, :, 128:256]
        v_ = xs.V[:, :, 256:384]

        def vsl(n, h):
            return xs.V[:, n, 256 + h * 64:256 + (h + 1) * 64]

        th = P.tmp("rth", [C, NB, 64], n=1)
        s.act(th.V[:], xs.V[:, :, 384 + d * 64:448 + d * 64], AF.Tanh)
        thT = P.tmp("rthT", [C, NB, C], n=1)
        a1T = P.tmp("ra1T", [C, NB, C], n=1)
        pm = K.ps.next()
        pmb = K.ps.next()
        for n in range(NB):
            s.tr(pm.V[0:C, n * C:(n + 1) * C], th.V[:, n, :], K.ident.V[0:C, 0:C])
            s.tr(pmb.V[0:C, n * C:(n + 1) * C], xs.V[:, n, 512 + d * 64:576 + d * 64], K.ident.V[0:C, 0:C])
        s.cp("act", rv(thT.V[:], "p n i -> p (n i)"), pm.V[0:C, 0:NB * C])
        s.cp("dve", rv(a1T.V[:], "p n i -> p (n i)"), pmb.V[0:C, 0:NB * C])
        pmw = K.ps.next()
        pma = K.ps.next()
        for n in range(NB):
            s.mm(pmw.V[0:C, n * 128:(n + 1) * 128], thT.V[:, n, :], w2.V[:, d, :])
            s.mm(pma.V[0:C, n * 128:(n + 1) * 128], a1T.V[:, n, :], a2.V[:, d, :])
        ld = P.tmp("rld", [C, NB, 128], n=1)
        s.tt("dve", ld.V[:], rv(pmw.V[0:C, 0:NB * 128], "p (n c) -> p n c", c=128), ins(vec.V[:, d, :], 1, NB), A.add)
        s.act(ld.V[:], ld.V[:], AF.Sigmoid)
        s.ts("dve", ld.V[:], ld.V[:], -0.6065306597126334, A.mult)
        aa = P.tmp("raa", [C, NB, 128], n=1)
        s.tt("dve", aa.V[:], rv(pma.V[0:C, 0:NB * 128], "p (n c) -> p n c", c=128), ins(vec.V[:, 2 + d, :], 1, NB), A.add)
        s.act(aa.V[:], aa.V[:], AF.Sigmoid)
        kx = P.tmp("rkx", [C, NB, 128], n=1)
        s.tt("pool", kx.V[:], k_, ins(vec.V[:, 4, :], 1, NB), A.mult)
        sq = P.tmp("rsq", [C, NB, 128], n=1)
        s.tt("pool", sq.V[:], kx.V[:], kx.V[:], A.mult)
        ssq = P.tmp("rssq", [C, NB, 2], n=1)
        s.red("dve", ssq.V[:], rv(sq.V[:], "p n (h f) -> p n h f", h=2))
        s.ts("dve", ssq.V[:], ssq.V[:], 1e-6, A.add)
        s.act(ssq.V[:], ssq.V[:], AF.Sqrt)
        rs = P.tmp("rrs", [C, NB, 2], n=1)
        P.op("dve", lambda e, rs=rs, ssq=ssq: e.reciprocal(out=rs[:], in_=ssq[:]), reads=[ssq], writes=[rs])
        kk = P.tmp("rkk", [C, NB, 128], n=1)
        s.tt("dve", rv(kk.V[:], "p n (h f) -> p n h f", h=2), rv(kx.V[:], "p n (h f) -> p n h f", h=2),
             ins(rs.V[:], 3, 64), A.mult)
        kd = P.tmp("rkd", [C, NB, 128], n=1)
        s.stt("dve", kd.V[:], aa.V[:], -1.0, ins(vec.V[:, 5, :], 1, NB), A.add, A.mult)
        s.ts("pool", kd.V[:], kd.V[:], 1.0, A.add)
        s.tt("pool", kd.V[:], kd.V[:], k_, A.mult)
        bb = P.tmp("rbb", [C, NB, 128], n=1)
        s.tt("pool", bb.V[:], kk.V[:], aa.V[:], A.mult)
        bt = P.tmp("rbt", [C, NB, 128], n=1)
        s.tt("pool", bt.V[:], r_, ins(vec.V[:, 6, :], 1, NB), A.mult)
        s.tt("pool", bt.V[:], bt.V[:], kd.V[:], A.mult)
        bsum = P.tmp("rbsum", [C, NB, 2], n=2)
        s.red("dve", bsum.V[:], rv(bt.V[:], "p n (h f) -> p n h f", h=2))
        ldf = rv(ld.V[:], "p n c -> p (n c)")
        pmG = K.ps.next()
        s.mm(pmG.V[0:C, 0:NB * 128], K.minc[d].V[:], ldf)
        pmT = K.ps.next()
        s.mm(pmT.V[0:C, 0:NB * 128], K.ones.V[:, 0:C], ldf)
        G = P.tmp("rG", [C, NB * 128], n=1)
        s.cp("act", G.V[:], pmG.V[0:C, 0:NB * 128])
        eG = P.tmp("reG", [C, NB * 128], n=1)
        emG = P.tmp("remG", [C, NB * 128], n=1)
        eGx = P.tmp("reGx", [C, NB * 128], n=1)
        eGt = P.tmp("reGt", [C, NB * 128], n=1)
        s.act(eG.V[:], G.V[:], AF.Exp)
        s.act(emG.V[:], G.V[:], AF.Exp, scale=-1.0)
        s.tt("dve", eGx.V[:], G.V[:], ldf, A.subtract)
        s.act(eGx.V[:], eGx.V[:], AF.Exp)
        s.tt("dve", eGt.V[:], pmT.V[0:C, 0:NB * 128], G.V[:], A.subtract)
        s.act(eGt.V[:], eGt.V[:], AF.Exp)
        pmC = K.ps.next()
        for n in range(NB):
            for h in range(2):
                m = 2 * n + h
                s.mm(pmC.V[0:C, m:m + 1], ld.V[:, n, h * 64:(h + 1) * 64], K.ones.V[:, 0:1])
        gC = P.tmp("rgC", [C, NM], n=2)
        s.act(gC.V[:], pmC.V[0:C, 0:NM], AF.Exp)
        fl = "p n c -> p (n c)"
        At = P.tmp("rAt", [C, NB, 128], n=1)
        Bt = P.tmp("rBt", [C, NB, 128], n=1)
        Kt = P.tmp("rKt", [C, NB, 128], n=1)
        Rt = P.tmp("rRt", [C, NB, 128], n=1)
        Bh = P.tmp("rBh", [C, NB, 128], n=2)
        Kh = P.tmp("rKh", [C, NB, 128], n=2)
        s.stt("dve", rv(At.V[:], fl), rv(kk.V[:], fl), -1.0, eGx.V[:], A.mult, A.mult)
        s.tt("pool", rv(Bt.V[:], fl), rv(bb.V[:], fl), emG.V[:], A.mult)
        s.tt("dve", rv(Kt.V[:], fl), rv(kd.V[:], fl), emG.V[:], A.mult)
        s.tt("pool", Rt.V[:], r_, rv(eG.V[:], "p (n c) -> p n c", c=128), A.mult)
        s.tt("dve", rv(Bh.V[:], fl), rv(bb.V[:], fl), eGt.V[:], A.mult)
        s.tt("pool", rv(Kh.V[:], fl), rv(kd.V[:], fl), eGt.V[:], A.mult)
        TTs = {}
        for name, src, eng in (("AtT", At, "act"), ("BtT", Bt, "dve"), ("KtT", Kt, "act"), ("RtT", Rt, "dve")):
            pm = K.ps.next()
            for n in range(NB):
                for h in range(2):
                    m = 2 * n + h
                    s.tr(pm.V[0:C, m * C:(m + 1) * C], src.V[:, n, h * 64:(h + 1) * 64], K.ident.V[0:C, 0:C])
            dst = P.tmp("r" + name, [C, NM, C], n=2)
            s.cp(eng, rv(dst.V[:], "p m i -> p (m i)"), pm.V[0:C, 0:NM * C])
            TTs[name] = dst
        AtT, BtT, KtT, RtT = TTs["AtT"], TTs["BtT"], TTs["KtT"], TTs["RtT"]
        mats = {}
        for name, l_, r__, mask in (("N", BtT, AtT, K.mstr[d]), ("NT", AtT, BtT, K.mstr[1 - d]),
                                    ("AakT", KtT, AtT, K.mstr[d]), ("ArbT", BtT, RtT, K.minc[d]),
                                    ("ArkT", KtT, RtT, K.minc[d])):
            pm = K.ps.next()
            for m in range(NM):
                s.mm(pm.V[0:C, m * C:(m + 1) * C], l_.V[:, m, :], r__.V[:, m, :])
            dst = P.tmp("r" + name, [C, NM, C], n=2)
            s.tt("dve", dst.V[:], rv(pm.V[0:C, 0:NM * C], "p (m i) -> p m i", i=C), ins(mask.V[:], 1, NM), A.mult)
            mats[name] = dst
        TT = inverse_doubling(s, K, mats["N"], mats["NT"], NM)
        pmX = K.ps.next()
        for n in range(NB):
            for h in range(2):
                m = 2 * n + h
                s.mm(pmX.V[0:C, m * C:(m + 1) * C], mats["AakT"].V[:, m, :], vsl(n, h))
        X1 = P.tmp("rX1", [C, NM, C], n=1)
        s.cp("act", rv(X1.V[:], "p m i -> p (m i)"), pmX.V[0:C, 0:NM * C])
        pmU = K.ps.next()
        pmW = K.ps.next()
        for n in range(NB):
            for h in range(2):
                m = 2 * n + h
                s.mm(pmU.V[0:C, m * C:(m + 1) * C], TT.V[:, m, :], X1.V[:, m, :])
                s.mm(pmW.V[0:C, m * C:(m + 1) * C], At.V[:, n, h * 64:(h + 1) * 64], TT.V[:, m, :])
        Uv = P.tmp("rUv", [C, NM, C], n=2)
        WmT = P.tmp("rWmT", [C, NM, C], n=2)
        s.cp("act", rv(Uv.V[:], "p m i -> p (m i)"), pmU.V[0:C, 0:NM * C])
        s.cp("dve", rv(WmT.V[:], "p m i -> p (m i)"), pmW.V[0:C, 0:NM * C])
        return dict(d=d, b=b, first=first, WmT=WmT, Uv=Uv, RtT=RtT, mats=mats, xs=xs, Bh=Bh, Kh=Kh, gC=gC, bsum=bsum)

    def steps(c_):
        d, b = c_["d"], c_["b"]
        WmT = c_["WmT"]
        Uv = c_["Uv"]
        RtT = c_["RtT"]
        mats = c_["mats"]
        xs = c_["xs"]
        Bh = c_["Bh"]
        Kh = c_["Kh"]
        gC = c_["gC"]
        bsum = c_["bsum"]
        v_ = xs.V[:, :, 256:384]

        def vsl(n, h):
            return xs.V[:, n, 256 + h * 64:256 + (h + 1) * 64]

        if c_["first"]:
            for h in range(2):
                s.memset("dve", H[h].V[:], 0.0)
        O = P.tmp("rO", [C, NB, 130], n=2)
        t0 = b * NB * C
        ofv = ofs.t.ap()[t0:t0 + NB * C, :].rearrange("(n j) c -> j n c", j=C)
        if d == 1:
            P.dma("act", O[:], ofv, reads=[ofs], writes=[O])
        for n in (range(NB) if d == 0 else range(NB - 1, -1, -1)):
            pm1 = K.ps2.next()
            for h in range(2):
                s.mm(pm1.V[0:C, h * C:(h + 1) * C], WmT.V[:, 2 * n + h, :], H[h].V[:])
            U = P.tmp("rU", [C, 128], n=3)
            s.tt("dve", U.V[:], rv(Uv.V[:, 2 * n:2 * n + 2, :], "p m i -> p (m i)"), pm1.V[0:C, 0:128], A.add)
            pm2 = K.ps2.next()
            for h in range(2):
                m = 2 * n + h
                s.mm(pm2.V[0:C, h * C:(h + 1) * C], RtT.V[:, m, :], H[h].V[:], start=True, stop=False)
                s.mm(pm2.V[0:C, h * C:(h + 1) * C], mats["ArbT"].V[:, m, :], U.V[:, h * C:(h + 1) * C], start=False, stop=False)
                s.mm(pm2.V[0:C, h * C:(h + 1) * C], mats["ArkT"].V[:, m, :], vsl(n, h), start=False, stop=True)
            if d == 0:
                s.cp("act", O.V[:, n, 0:128], pm2.V[0:C, 0:128])
            else:
                s.tt("dve", O.V[:, n, 0:128], O.V[:, n, 0:128], pm2.V[0:C, 0:128], A.add)
            pm3 = K.ps2.next()
            for h in range(2):
                s.mm(pm3.V[0:C, h * C:(h + 1) * C], Bh.V[:, n, h * C:(h + 1) * C], U.V[:, h * C:(h + 1) * C], start=True, stop=False)
                s.mm(pm3.V[0:C, h * C:(h + 1) * C], Kh.V[:, n, h * C:(h + 1) * C], vsl(n, h), start=False, stop=True)
            for h in range(2):
                m = 2 * n + h
                s.stt("dve", H[h].V[:], H[h].V[:], gC.V[:, m:m + 1], pm3.V[0:C, h * C:(h + 1) * C], A.mult, A.add)
        if d == 0:
            s.cp("dve", O.V[:, :, 128:130], bsum.V[:])
            P.dma("act", ofv, O[:], reads=[O], writes=[ofs])
        else:
            s.tt("dve", O.V[:, :, 128:130], O.V[:, :, 128:130], bsum.V[:], A.add)
            sg = P.tmp("rsg", [C, NB, 160], n=1)
            s.act(sg.V[:], xs.V[:, :, 640:800], AF.Sigmoid)
            pma_ = K.ps2.next()
            pmb_ = K.ps2.next()
            for n in range(NB):
                s.tr(pma_.V[0:128, n * C:(n + 1) * C], sg.V[:, n, 0:128], K.ident.V[0:C, 0:C])
                s.tr(pmb_.V[0:32, n * C:(n + 1) * C], sg.V[:, n, 128:160], K.ident.V[0:C, 0:C])
            sgTa = P.tmp("rsgTa", [128, NB, C], n=1)
            sgTb = P.tmp("rsgTb", [32, NB, C], n=1)
            s.cp("act", rv(sgTa.V[:], "p n i -> p (n i)"), pma_.V[0:128, 0:NB * C])
            s.cp("dve", rv(sgTb.V[:], "p n i -> p (n i)"), pmb_.V[0:32, 0:NB * C])
            pmg = K.ps2.next()
            for n in range(NB):
                s.mm(pmg.V[0:C, n * 128:(n + 1) * 128], sgTa.V[:, n, :], g2a.V[:], start=True, stop=False)
                s.mm(pmg.V[0:C, n * 128:(n + 1) * 128], sgTb.V[:, n, :], g2b.V[:], start=False, stop=True)
            o4 = rv(O.V[:, :, 0:128], "p n (h f) -> p n h f", h=2)
            mean = P.tmp("rmean", [C, NB, 2], n=1)
            s.red("dve", mean.V[:], o4)
            s.ts("dve", mean.V[:], mean.V[:], 1.0 / 64, A.mult)
            xc = P.tmp("rxc", [C, NB, 128], n=1)
            xc4 = rv(xc.V[:], "p n (h f) -> p n h f", h=2)
            s.tt("dve", xc4, o4, ins(mean.V[:], 3, 64), A.subtract)
            sq2 = P.tmp("rsq2", [C, NB, 128], n=1)
            s.tt("pool", sq2.V[:], xc.V[:], xc.V[:], A.mult)
            var = P.tmp("rvar", [C, NB, 2], n=1)
            s.red("dve", var.V[:], rv(sq2.V[:], "p n (h f) -> p n h f", h=2))
            s.ts("dve", var.V[:], var.V[:], 1.0 / 64, A.mult, 64e-5, A.add)
            s.act(var.V[:], var.V[:], AF.Sqrt)
            rstd = P.tmp("rrstd", [C, NB, 2], n=1)
            P.op("dve", lambda e, rstd=rstd, var=var: e.reciprocal(out=rstd[:], in_=var[:]), reads=[var], writes=[rstd])
            s.tt("dve", xc4, xc4, ins(rstd.V[:], 3, 64), A.mult)
            s.tt("pool", xc.V[:], xc.V[:], ins(vec.V[:, 7, :], 1, NB), A.mult)
            s.tt("pool", xc.V[:], xc.V[:], ins(vec.V[:, 8, :], 1, NB), A.add)
            bon = P.tmp("rbon", [C, NB, 128], n=1)
            s.tt("dve", rv(bon.V[:], "p n (h f) -> p n h f", h=2), rv(v_, "p n (h f) -> p n h f", h=2),
                 ins(O.V[:, :, 128:130], 3, 64), A.mult)
            s.tt("pool", xc.V[:], xc.V[:], bon.V[:], A.add)
            y = P.tmp("ry", [C, NB, 128], n=2)
            s.tt("dve", y.V[:], xc.V[:], rv(pmg.V[0:C, 0:NB * 128], "p (n c) -> p n c", c=128), A.mult)
            P.dma("sp", orr.t.ap()[t0:t0 + NB * C, ocol:ocol + 128].rearrange("(n j) c -> j n c", j=C), y[:], reads=[y],
                  is_output=is_out)

    seq = []
    for d in range(2):
        order = list(range(NBLK)) if d == 0 else [0] + list(range(NBLK - 1, 0, -1))
        for j_, b in enumerate(order):
            seq.append((d, b, j_ == 0))
    pipeline_blocks(P, seq, prep, steps)


def build_rwkv_test():
    nc = new_nc()
    P = Prog(nc)
    Pr = P.dram("Pr", [RW_ROWS, RW], F32, "ExternalInput")
    rmu = P.dram("rmu", [RW], F32, "ExternalInput")
    rvec = P.dram("rvec", [9, 128], F32, "ExternalInput")
    rw2 = P.dram("rw2", [2, 64, 128], F32, "ExternalInput")
    ra2 = P.dram("ra2", [2, 64, 128], F32, "ExternalInput")
    rg2 = P.dram("rg2", [160, 128], F32, "ExternalInput")
    orr = P.dram("orr", [NCH * C, 128], F32, "ExternalOutput")
    K = ScanConsts(P)
    emit_rwkv(P, K, Pr, rmu, rvec, rw2, ra2, rg2, orr)
    P.emit()
    return nc


def rwkv_cols(i):
    h = np.arange(128)
    return np.concatenate([i * 128 + h, 1024 + i * 128 + h, 2048 + i * 128 + h,
                           3072 + np.arange(256), 3328 + np.arange(160)]).astype(np.int64)


NTALL = 66
MW = GW + RW


def emit_proj_all(P, xall, vecs, wsl, Pg, Pr, nv=None, latall=None):
    idf, _ = make_ident(P)
    if nv is None:
        vt = P.sb("vt", [128, 5, KT])
        P.dma("sp", vt[:], vecs[:], writes=[vt])
        for j in (1, 3):
            P.op("dve", lambda e, j=j: e.scalar_tensor_tensor(out=vt[:, j, :], in0=vt[:, j, :], scalar=1.0,
                                                              in1=vt[:, 0, :], op0=ALU.add, op1=ALU.mult),
                 reads=[vt], writes=[vt])
        slots = {"l": (1, 2), "c": (3, 4)}
    else:
        vt = nv
        slots = {"l": (0, 1), "c": (2, 3)}
    slab = P.sb("wslab", [128, KT, MW], BF16)
    wv = wsl.t.ap().rearrange("(k p) n -> p k n", p=128)
    for k4 in range(4):
        P.dma("pool", slab[:, k4 * 4:(k4 + 1) * 4, :], wv[:, k4 * 4:(k4 + 1) * 4, :], writes=[slab])
    zt = P.sb("zt", [64, RW])
    P.op("dve", lambda e: e.memset(zt[:], 0.0), writes=[zt])
    for (r0, n) in ((0, 1), (257, 2), (8451, 1)):
        P.dma("sp", Pg.t.ap()[r0:r0 + n, :], zt[0:n, 0:GW], reads=[zt], writes=[Pg])
    for (r0, n) in ((0, 1), (257, 64), (8513, 64)):
        P.dma("sp", Pr.t.ap()[r0:r0 + n, :], zt[0:n, :], reads=[zt], writes=[Pr])
    xbufs = Rot([P.sb("x", [128, D]) for _ in range(2)])
    xn = P.sb("xn", [128, D])
    tmp = P.sb("tmp", [128, KT, 128])
    ptr = None
    idb = P.sb("idb", [128, 128], BF16)
    P.op("dve", lambda e: e.tensor_copy(out=idb[:], in_=idf[:]), reads=[idf], writes=[idb])
    xnbs = Rot([P.sb("xnb", [128, D], BF16) for _ in range(2)])
    ptrbs = Rot([P.ps("ptrb", [128, KT, 128], BF16) for _ in range(2)])
    pms = Rot([P.ps("pm", [128, 512]) for _ in range(4)])
    hTs = Rot([P.sb("hT", [128, KT, 128], BF16) for _ in range(2)])
    obg = Rot([P.sb("obg", [128, GW]) for _ in range(2)])
    obr = Rot([P.sb("obr", [128, RW]) for _ in range(2)])
    sss = Rot([P.sb("ss", [128, 1]) for _ in range(2)])
    rss = Rot([P.sb("rs", [128, 1]) for _ in range(2)])
    for t in range(NTALL):
        xs = xbufs.next()
        q = "sp" if t % 2 == 0 else "act"
        if latall is None:
            P.dma(q, xs[:], xall[t], writes=[xs])
        elif t < 2:
            for r4 in range(4):
                rk = 4 * t + r4
                P.dma(q, xs[r4 * 32:(r4 + 1) * 32, :], latall.t.ap()[rk * 1056 + 1024:rk * 1056 + 1056, :], writes=[xs])
        else:
            g_ = t - 2
            r0_ = (g_ // 8) * 1056 + (g_ % 8) * 128
            P.dma(q, xs[:], latall.t.ap()[r0_:r0_ + 128, :], writes=[xs])
        jg, jb = slots["c"] if t < 2 else slots["l"]
        hT = hTs.next()
        emit_norm_T(P, xs, hT, 0, vt, jg, jb, idf, ptr, xn, tmp, sss.next(), rss.next(),
                    bt=(xnbs.next(), idb, ptrbs.next()))
        og_, or_ = obg.next(), obr.next()
        for (c0, ncol, ob, oc0, eng) in ((0, 512, og_, 0, "act"), (512, 4, og_, 512, "dve"),
                                         (GW, 512, or_, 0, "act"), (GW + 512, RW - 512, or_, 512, "dve")):
            pm = pms.next()
            for k in range(KT):
                P.op("pe", lambda e, k=k, pm=pm, hT=hT, c0=c0, ncol=ncol: e.matmul(
                    pm[:, :ncol], lhsT=hT[:, k, :], rhs=slab[:, k, c0:c0 + ncol],
                    start=(k == 0), stop=(k == KT - 1)), reads=[hT, slab], writes=[pm])
            if eng == "act":
                P.op("act", lambda e, pm=pm, ob=ob, oc0=oc0, ncol=ncol: e.copy(out=ob[:, oc0:oc0 + ncol], in_=pm[:, :ncol]),
                     reads=[pm], writes=[ob])
            else:
                P.op("dve", lambda e, pm=pm, ob=ob, oc0=oc0, ncol=ncol: e.tensor_copy(out=ob[:, oc0:oc0 + ncol], in_=pm[:, :ncol]),
                     reads=[pm], writes=[ob])
        if t < 2:
            rg = 1 + t * 128
            rr = 1 + t * 128
        else:
            rg = 259 + (t - 2) * 128
            rr = RW_LAT0 + (t - 2) * 128
        P.dma("sp", Pg.t.ap()[rg:rg + 128, :], og_[:], reads=[og_], writes=[Pg])
        P.dma("act", Pr.t.ap()[rr:rr + 128, :], or_[:], reads=[or_], writes=[Pr])


def build_mixer():
    nc = new_nc()
    P = Prog(nc)
    xall = P.dram("xall", [NTALL, 128, D], F32, "ExternalInput")
    vecs = P.dram("vecs", [128, 5, KT], F32, "ExternalInput")
    wsl = P.dram("wsl", [D, MW], F32, "ExternalInput")
    gcw = P.dram("gcw", [3, 384], F32, "ExternalInput")
    gsc = P.dram("gsc", [4], F32, "ExternalInput")
    gng = P.dram("gng", [128], F32, "ExternalInput")
    rmu = P.dram("rmu", [RW], F32, "ExternalInput")
    rvec = P.dram("rvec", [9, 128], F32, "ExternalInput")
    rw2 = P.dram("rw2", [2, 64, 128], F32, "ExternalInput")
    ra2 = P.dram("ra2", [2, 64, 128], F32, "ExternalInput")
    rg2 = P.dram("rg2", [160, 128], F32, "ExternalInput")
    og = P.dram("og", [NCH * C, 128], F32, "ExternalOutput")
    orr = P.dram("orr", [NCH * C, 128], F32, "ExternalOutput")
    Pg = P.dram("Pg", [GDN_ROWS, GW], F32)
    Pr = P.dram("Pr", [RW_ROWS, RW], F32)
    P.push_scope()
    emit_proj_all(P, xall, vecs, wsl, Pg, Pr)
    P.pop_scope()
    P.push_scope()
    K = ScanConsts(P)
    emit_gdn(P, K, Pg, gcw, gsc, gng, og)
    P.pop_scope()
    P.push_scope()
    K = ScanConsts(P)
    emit_rwkv(P, K, Pr, rmu, rvec, rw2, ra2, rg2, orr)
    P.pop_scope()
    P.emit()
    return nc


_PROGS = {}


def _prog(name):
    if name not in _PROGS:
        _PROGS[name] = {"mod": build_mod, "mixer": build_mixer, "ffn0": lambda: build_ffn(False),
                        "ffn1": lambda: build_ffn(True)}[name]()
    return _PROGS[name]


def _featT(v):
    return np.ascontiguousarray(np.asarray(v, np.float32).reshape(KT, 128).T)


def _run(nc, in_maps):
    from concourse.bass_utils import run_bass_kernel_spmd
    return run_bass_kernel_spmd(nc, in_maps, core_ids=list(range(NCORE))).results


def mixer_in_maps(l, lat, cx, mod_l, mod_c, inp):
    sh1_l, sc1_l = mod_l[0:D], mod_l[D:2 * D]
    sh1_c, sc1_c = mod_c[0:D], mod_c[D:2 * D]
    vecs = np.stack([_featT(inp["norm1_g"][l]), _featT(sc1_l), _featT(sh1_l), _featT(sc1_c), _featT(sh1_c)],
                    axis=1).astype(np.float32)
    xall = np.ascontiguousarray(np.concatenate([cx, lat], axis=0).reshape(NTALL, 128, D))
    w_in = inp["w_in"][l]
    maps = []
    for i in range(NCORE):
        sl = slice(i * 128, (i + 1) * 128)
        cols = np.concatenate([gdn_cols(i), P_GDN + rwkv_cols(i)])
        rvec = np.stack([inp["rwkv_w0"][l][0, sl], inp["rwkv_w0"][l][1, sl], inp["rwkv_a0"][l][0, sl],
                         inp["rwkv_a0"][l][1, sl], inp["rwkv_k_k"][l][sl], inp["rwkv_k_a"][l][sl],
                         inp["rwkv_r_k"][l][sl], inp["rwkv_ln_g"][l][sl], inp["rwkv_ln_b"][l][sl]]).astype(np.float32)
        gsc = np.array([inp["gdn_a_log"][l][0, i], inp["gdn_a_log"][l][1, i], inp["gdn_dt_bias"][l][0, i],
                        inp["gdn_dt_bias"][l][1, i]], np.float32)
        maps.append({
            "xall": xall, "vecs": vecs, "wsl": np.ascontiguousarray(w_in[:, cols]),
            "gcw": np.ascontiguousarray(inp["gdn_conv_w"][l][:, gdn_conv_cols(i)]), "gsc": gsc,
            "gng": np.ascontiguousarray(inp["gdn_norm_g"][l]),
            "rmu": np.ascontiguousarray(inp["rwkv_mu"][l][rwkv_cols(i)]), "rvec": rvec,
            "rw2": np.ascontiguousarray(inp["rwkv_w2"][l][:, :, sl]),
            "ra2": np.ascontiguousarray(inp["rwkv_a2"][l][:, :, sl]),
            "rg2": np.ascontiguousarray(inp["rwkv_g2"][l][:, sl]),
        })
    return maps


def mixer_gather(res):
    o_l = np.empty((8192, D), np.float32)
    o_c = np.empty((256, D), np.float32)
    for i in range(NCORE):
        og = res[i]["og"]
        orr = res[i]["orr"]
        o_c[:, i * 128:(i + 1) * 128] = og[:256]
        o_l[:, i * 128:(i + 1) * 128] = og[256:]
        o_c[:, 1024 + i * 128:1024 + (i + 1) * 128] = orr[:256]
        o_l[:, 1024 + i * 128:1024 + (i + 1) * 128] = orr[256:].reshape(64, 128, 128).transpose(1, 0, 2).reshape(8192, 128)
    return o_l, o_c


def ffn_in_maps(l, o_l, o_c, lat, cx, mod_l, mod_c, inp):
    gt1_l, sh2_l, sc2_l, gt2_l = mod_l[2 * D:3 * D], mod_l[3 * D:4 * D], mod_l[4 * D:5 * D], mod_l[5 * D:6 * D]
    gt1_c, sh2_c, sc2_c, gt2_c = mod_c[2 * D:3 * D], mod_c[3 * D:4 * D], mod_c[4 * D:5 * D], mod_c[5 * D:6 * D]
    vecs = np.stack([_featT(inp["norm2_g"][l]), _featT(sc2_l), _featT(sh2_l), _featT(sc2_c), _featT(sh2_c)],
                    axis=1).astype(np.float32)
    gates = np.stack([gt1_l, gt1_c, gt2_l, gt2_c, np.asarray(inp["final_g"], np.float32)]).astype(np.float32)
    wout = np.ascontiguousarray(inp["w_out"][l])
    w1 = np.ascontiguousarray(inp["mlp_w1"][l])
    w2 = np.ascontiguousarray(inp["mlp_w2"][l])
    maps = []
    for i in range(NCORE):
        oin = np.zeros((NT, 128, D), np.float32)
        oin[:NLAT] = o_l[i * 1024:(i + 1) * 1024].reshape(NLAT, 128, D)
        oin[NLAT, :32] = o_c[i * 32:(i + 1) * 32]
        latin = np.zeros((NT, 128, D), np.float32)
        latin[:NLAT] = lat[i * 1024:(i + 1) * 1024].reshape(NLAT, 128, D)
        latin[NLAT, :32] = cx[i * 32:(i + 1) * 32]
        maps.append({"oin": oin, "latin": latin, "vecs": vecs, "gates": gates, "wout": wout, "w1": w1, "w2": w2})
    return maps


def kernel_unfused(**inputs):
    inp = {k: np.asarray(v) for k, v in inputs.items()}
    cT = np.stack([_featT(inp["c"][0]), _featT(inp["c_ctx"])], axis=-1).astype(np.float32)
    maps = []
    for i in range(NCORE):
        sl = slice(i * MODC, (i + 1) * MODC)
        maps.append({"cT": cT, "aw": np.ascontiguousarray(inp["ada_w"][:, :, sl]),
                     "ab": np.ascontiguousarray(inp["ada_b"][:, sl])})
    res = _run(_prog("mod"), maps)
    mod = np.concatenate([res[i]["mod"] for i in range(NCORE)], axis=-1)
    lat = np.ascontiguousarray(inp["x"][0], dtype=np.float32)
    cx = np.ascontiguousarray(inp["ctx"][0], dtype=np.float32)
    for l in range(2):
        mod_l, mod_c = mod[l, 0], mod[l, 1]
        res = _run(_prog("mixer"), mixer_in_maps(l, lat, cx, mod_l, mod_c, inp))
        o_l, o_c = mixer_gather(res)
        res = _run(_prog("ffn1" if l == 1 else "ffn0"), ffn_in_maps(l, o_l, o_c, lat, cx, mod_l, mod_c, inp))
        lat = np.concatenate([res[i]["latout"][:NLAT].reshape(NLAT * 128, D) for i in range(NCORE)], axis=0)
        cx = np.concatenate([res[i]["latout"][NLAT, :32] for i in range(NCORE)], axis=0)
    return lat.reshape(1, 8192, D).astype(np.float32)


OROWS = NCH * C
LROWS = 1056
MODW = 6 * D


def emit_mod_full(P, cT, aw, ab, normg, modrow, NV):
    s = SB(P)
    idf, _ = make_ident(P)
    cs = P.sb("cs", [128, KT, 2])
    sl = P.sb("s", [128, KT, 2])
    P.dma("sp", cs[:], cT[:], writes=[cs])
    P.op("act", lambda e: e.activation(out=sl[:], in_=cs[:], func=AF.Silu), reads=[cs], writes=[sl])
    bia = P.sb("bia", [2, 2, MODW])
    P.dma("sp", bia[:], bass.AP(ab.t, 0, [[0, 2], [MODW, 2], [1, MODW]]), writes=[bia])
    ng = P.sb("ng", [128, 2, 2, KT])
    P.dma("sp", ng[:], normg[:], writes=[ng])
    vta = [P.sb("vta", [128, 96, 2]) for _ in range(2)]
    slabs = Rot([P.sb("slab", [128, KT, 512]) for _ in range(2)])
    pms = Rot([P.ps("pm", [128, 512]) for _ in range(3)])
    pts = Rot([P.ps("pt", [128, 512]) for _ in range(2)])
    obs = Rot([P.sb("ob", [2, 512]) for _ in range(3)])
    for l in range(2):
        wv = aw.t.ap()[l].rearrange("(k p) n -> p k n", p=128)
        for b in range(MODW // 512):
            slab = slabs.next()
            P.dma("pool", slab[:], wv[:, :, b * 512:(b + 1) * 512], writes=[slab])
            pm = pms.next()
            for k in range(KT):
                P.op("pe", lambda e, k=k, pm=pm, slab=slab: e.matmul(pm[0:2, :], lhsT=sl[:, k, :], rhs=slab[:, k, :],
                                                                     start=(k == 0), stop=(k == KT - 1)),
                     reads=[sl, slab], writes=[pm])
            ob = obs.next()
            P.op("dve", lambda e, pm=pm, ob=ob, l=l, b=b: e.tensor_tensor(
                out=ob[:], in0=pm[0:2, :], in1=bia[:, l, b * 512:(b + 1) * 512], op=ALU.add),
                 reads=[pm, bia], writes=[ob])
            P.dma("sp", modrow.t.ap()[l][:, b * 512:(b + 1) * 512], ob[:], reads=[ob])
            pt = pts.next()
            for kk in range(4):
                P.op("pe", lambda e, kk=kk, pt=pt, ob=ob: e.transpose(out=pt[:, kk * 2:kk * 2 + 2],
                                                                      in_=ob[0:2, kk * 128:(kk + 1) * 128],
                                                                      identity=idf[0:2, 0:2]),
                     reads=[ob, idf], writes=[pt])
            P.op("act", lambda e, pt=pt, l=l, b=b: e.copy(out=vta[l][:, b * 4:(b + 1) * 4, :],
                                                          in_=pt[:, 0:8].rearrange("p (k j) -> p k j", j=2)),
                 reads=[pt], writes=[vta[l]])
        for which, (m_sh, m_sc) in enumerate(((0, 1), (3, 4))):
            for j in range(2):
                sc_v = vta[l].V[:, m_sc * 16:(m_sc + 1) * 16, j]
                sh_v = vta[l].V[:, m_sh * 16:(m_sh + 1) * 16, j]
                slot = which * 4 + j * 2
                s.stt("dve", NV[l].V[:, slot, :], sc_v, 1.0, ng.V[:, l, which, :], ALU.add, ALU.mult)
                s.cp("dve", NV[l].V[:, slot + 1, :], sh_v)


def emit_mod_shard(P, cT, aw, ab, normg, modmine, modall, modrow, NV):
    s = SB(P)
    idf, _ = make_ident(P)
    cs = P.sb("cs", [128, KT, 2])
    sl = P.sb("s", [128, KT, 2])
    P.dma("sp", cs[:], cT[:], writes=[cs])
    P.op("act", lambda e: e.activation(out=sl[:], in_=cs[:], func=AF.Silu), reads=[cs], writes=[sl])
    bia = P.sb("bia", [2, 2, MODC])
    P.dma("sp", bia[:], bass.AP(ab.t, 0, [[0, 2], [MODC, 2], [1, MODC]]), writes=[bia])
    ng = P.sb("ng", [128, 2, 2, KT])
    P.dma("sp", ng[:], normg[:], writes=[ng])
    vta = [P.sb("vta", [128, 96, 2]) for _ in range(2)]
    slabs = Rot([P.sb("slab", [128, KT, 512]) for _ in range(2)])
    pms = Rot([P.ps("pm", [128, 512]) for _ in range(3)])
    pts = Rot([P.ps("pt", [128, 512]) for _ in range(2)])
    obs = Rot([P.sb("ob", [2, 512]) for _ in range(3)])
    for l in range(2):
        wv = aw.t.ap()[l].rearrange("(k p) n -> p k n", p=128)
        for b in range(MODC // 512):
            slab = slabs.next()
            P.dma("pool", slab[:], wv[:, :, b * 512:(b + 1) * 512], writes=[slab])
            pm = pms.next()
            for k in range(KT):
                P.op("pe", lambda e, k=k, pm=pm, slab=slab: e.matmul(pm[0:2, :], lhsT=sl[:, k, :], rhs=slab[:, k, :],
                                                                     start=(k == 0), stop=(k == KT - 1)),
                     reads=[sl, slab], writes=[pm])
            ob = obs.next()
            P.op("dve", lambda e, pm=pm, ob=ob, l=l, b=b: e.tensor_tensor(
                out=ob[:], in0=pm[0:2, :], in1=bia[:, l, b * 512:(b + 1) * 512], op=ALU.add),
                 reads=[pm, bia], writes=[ob])
            P.dma("sp", bass.AP(modmine.t, l * 2 * MODC + b * 512, [[MODC, 2], [1, 512]]), ob[:], reads=[ob])
    P.collective("AllGather", modmine.t.ap(), modall.t.ap())
    mall = P.sb("mall", [NCORE * 4, MODC])
    P.dma("sp", mall[:], bass.AP(modall.t, 0, [[MODC, NCORE * 4], [1, MODC]]), writes=[mall])
    for r in range(NCORE):
        P.dma("sp" if r % 2 == 0 else "act", bass.AP(modrow.t, r * MODC, [[MODW, 4], [1, MODC]]),
              mall[r * 4:(r + 1) * 4, :], reads=[mall])
    P.barrier()
    for l in range(2):
        for b in range(MODW // 512):
            ob = obs.next()
            P.dma("sp" if b % 2 == 0 else "act", ob[:], modrow.t.ap()[l][:, b * 512:(b + 1) * 512], writes=[ob])
            pt = pts.next()
            for kk in range(4):
                P.op("pe", lambda e, kk=kk, pt=pt, ob=ob: e.transpose(out=pt[:, kk * 2:kk * 2 + 2],
                                                                      in_=ob[0:2, kk * 128:(kk + 1) * 128],
                                                                      identity=idf[0:2, 0:2]),
                     reads=[ob, idf], writes=[pt])
            P.op("act", lambda e, pt=pt, l=l, b=b: e.copy(out=vta[l][:, b * 4:(b + 1) * 4, :],
                                                          in_=pt[:, 0:8].rearrange("p (k j) -> p k j", j=2)),
                 reads=[pt], writes=[vta[l]])
        for which, (m_sh, m_sc) in enumerate(((0, 1), (3, 4))):
            for j in range(2):
                sc_v = vta[l].V[:, m_sc * 16:(m_sc + 1) * 16, j]
                sh_v = vta[l].V[:, m_sh * 16:(m_sh + 1) * 16, j]
                slot = which * 4 + j * 2
                s.stt("dve", NV[l].V[:, slot, :], sc_v, 1.0, ng.V[:, l, which, :], ALU.add, ALU.mult)
                s.cp("dve", NV[l].V[:, slot + 1, :], sh_v)


def emit_select(P, oall, osel):
    oa = oall.t
    os_ = osel.t
    dq = "pool"
    for r2 in range(2):
        off = (256 + r2 * 512) * 256
        P.dma(dq, bass.AP(os_, r2 * 512 * D, [[128, NCORE], [D, 512], [1, 128]]),
              lambda pid, off=off: bass.AP(oa, pid * (1024 * 256) + off, [[OROWS * 256, NCORE], [256, 512], [1, 128]]))
    for i in range(NCORE):
        off = (i * OROWS + 256) * 256 + 128
        P.dma(dq, bass.AP(os_, 1024 + i * 128, [[D, 64], [64 * D, 16], [1, 128]]),
              lambda pid, off=off: bass.AP(oa, pid * (16 * 256) + off, [[128 * 256, 64], [256, 16], [1, 128]]))
    P.dma(dq, bass.AP(os_, 1024 * D, [[256, NCORE], [D, 32], [1, 256]]),
          lambda pid: bass.AP(oa, pid * (32 * 256), [[OROWS * 256, NCORE], [256, 32], [1, 256]]))


def emit_ffn(P, l, final, nv, modrow, fing, oall, oidx, lat_src, wout, w1, w2, dst):
    idf, _ = make_ident(P)
    vt = nv
    GT = P.sb("GT", [128, 2, D])

    def load_gates(m):
        P.dma("sp", GT[:], bass.AP(modrow.t, l * 2 * MODW + m * D, [[0, 128], [MODW, 2], [1, D]]), writes=[GT])

    load_gates(2)
    acc = [P.sb("acc", [128, D]) for _ in range(NT)]
    for t in range(NT):
        if lat_src[0] == "tiles":
            P.dma("act", acc[t][:], lat_src[1][t], writes=[acc[t]])
        else:
            P.dma("act", acc[t][:], lat_src[1].t.ap()[t * 128:(t + 1) * 128, :], writes=[acc[t]])
    hT = P.sb("hT", [128, KT, NTOK], BF16)
    xn = P.sb("xn", [128, D])
    otile = xn
    tmp = P.sb("tmp", [128, KT, 128])
    ptr = P.ps("ptr", [128, KT, 128])
    pms = Rot([P.ps("pm", [128, 512]) for _ in range(4)])
    tmp2s = Rot([P.sb("tmp2", [128, 512]) for _ in range(3)])
    slabs = Rot([P.sb("slab", [128, KT, 512], BF16) for _ in range(2)])
    zTs = Rot([P.sb("zT", [128, 4, NTOK], BF16) for _ in range(2)])
    ix = P.sb("oix", [128, NT * NCORE * 2], mybir.dt.int32)
    P.dma("sp", ix[:], oidx[:], writes=[ix])
    ochunks = bass.AP(oall.t, 0, [[128, 2 * NCORE * OROWS], [1, 128]])
    otb = [Buf("otb", xn.t) for _ in range(KT)]
    for t in range(NT):
        np_ = 128 if t < NLAT else 32
        if t >= NLAT:
            P.op("dve", lambda e: e.memset(xn[:], 0.0), writes=otb)
        for i in range(NCORE):
            for g_ in range(2):
                kb = g_ * 8 + i
                col = (t * NCORE + i) * 2 + g_
                P.idma(xn[0:np_, kb * 128:(kb + 1) * 128], ochunks, ix[0:np_, col:col + 1], reads=[ix], writes=[otb[kb]])
        for k in range(KT):
            P.op("pe", lambda e, k=k: e.transpose(out=ptr[:, k, :], in_=xn[:, k * 128:(k + 1) * 128], identity=idf[:]),
                 reads=[otb[k], idf], writes=[ptr])
        P.op("act", lambda e, t=t: e.copy(out=hT[:, 0:8, t * 128:(t + 1) * 128], in_=ptr[:, 0:8, :]),
             reads=[ptr], writes=[hT])
        P.op("dve", lambda e, t=t: e.tensor_copy(out=hT[:, 8:16, t * 128:(t + 1) * 128], in_=ptr[:, 8:16, :]),
             reads=[ptr], writes=[hT])

    def gated_acc(pm, t, c0, gslot):
        tmp2 = tmp2s.next()
        P.op("dve", lambda e: e.tensor_tensor(out=tmp2[:], in0=pm[:], in1=GT[:, gslot, c0:c0 + 512], op=ALU.mult),
             reads=[pm, GT], writes=[tmp2])
        P.op("dve", lambda e: e.tensor_tensor(out=acc[t][:, c0:c0 + 512], in0=acc[t][:, c0:c0 + 512], in1=tmp2[:],
                                              op=ALU.add), reads=[acc[t], tmp2], writes=[acc[t]])

    wv = wout.t.ap().rearrange("(k p) n -> p k n", p=128)
    for cb in range(4):
        c0 = cb * 512
        slab = slabs.next()
        P.dma("pool", slab[:], wv[:, :, c0:c0 + 512], writes=[slab])
        for t in range(NT):
            pm = pms.next()
            for k in range(KT):
                P.op("pe", lambda e, k=k, t=t, pm=pm, slab=slab: e.matmul(
                    pm[:], lhsT=hT[:, k, t * 128:(t + 1) * 128], rhs=slab[:, k, :],
                    start=(k == 0), stop=(k == KT - 1)), reads=[hT, slab], writes=[pm])
            gated_acc(pm, t, c0, 0 if t < NLAT else 1)
    sss = Rot([P.sb("ss", [128, 1]) for _ in range(2)])
    rss = Rot([P.sb("rs", [128, 1]) for _ in range(2)])
    for t in range(NT):
        jg, jb = (4, 5) if t < NLAT else (6, 7)
        emit_norm_T(P, acc[t], hT, t * 128, vt, jg, jb, idf, ptr, xn, tmp, sss.next(), rss.next())
    load_gates(5)
    w1v = w1.t.ap().rearrange("(k p) n -> p k n", p=128)
    groups = [(0, 512), (512, 512), (1024, 128)]
    nt_mlp = NT
    if final:
        groups = groups[:2]
        nt_mlp = NLAT
    rrs = Rot([P.sb("rr", [128, 512]) for _ in range(2)])
    for c in range(D_FF // 512):
        slabA = slabs.next()
        P.dma("pool", slabA[:], w1v[:, :, c * 512:(c + 1) * 512], writes=[slabA])
        zT = zTs.next()
        for f in range(4):
            for (g0, gn) in groups:
                pm = pms.next()
                for k in range(KT):
                    P.op("pe", lambda e, k=k, f=f, pm=pm, slabA=slabA, g0=g0, gn=gn: e.matmul(
                        pm[:, :gn], lhsT=slabA[:, k, f * 128:(f + 1) * 128], rhs=hT[:, k, g0:g0 + gn],
                        start=(k == 0), stop=(k == KT - 1)), reads=[hT, slabA], writes=[pm])
                rr = rrs.next()
                P.op("act", lambda e, pm=pm, rr=rr, gn=gn: e.activation(out=rr[:, :gn], in_=pm[:, :gn], func=AF.Relu),
                     reads=[pm], writes=[rr])
                P.op("dve", lambda e, rr=rr, zT=zT, f=f, g0=g0, gn=gn: e.tensor_tensor(
                    out=zT[:, f, g0:g0 + gn], in0=rr[:, :gn], in1=rr[:, :gn], op=ALU.mult), reads=[rr], writes=[zT])
        slabB = slabs.next()
        P.dma("pool", slabB.t.ap().rearrange("p (f b) n -> p f (b n)", f=4),
              w2.t.ap()[c * 512:(c + 1) * 512, :].rearrange("(f p) n -> p f n", p=128), writes=[slabB])
        for t in range(nt_mlp):
            for b in range(4):
                pm = pms.next()
                for f in range(4):
                    P.op("pe", lambda e, f=f, t=t, b=b, pm=pm, slabB=slabB, zT=zT: e.matmul(
                        pm[:], lhsT=zT[:, f, t * 128:(t + 1) * 128], rhs=slabB[:, f * 4 + b, :],
                        start=(f == 0), stop=(f == 3)), reads=[zT, slabB], writes=[pm])
                gated_acc(pm, t, b * 512, 0 if t < NLAT else 1)
    if final:
        P.dma("sp", GT[:, 0, :], bass.AP(fing.t, 0, [[0, 128], [1, D]]), writes=[GT])
    for t in range(NT):
        if final:
            if t >= NLAT:
                continue
            ss = sss.next()
            rs = rss.next()
            emit_rstd(P, acc[t], xn, ss, rs)
            P.op("dve", lambda e, t=t, rs=rs: e.scalar_tensor_tensor(out=acc[t][:], in0=acc[t][:], scalar=rs[:, 0:1],
                                                                     in1=GT[:, 0, :], op0=ALU.mult, op1=ALU.mult),
                 reads=[acc[t], rs, GT], writes=[acc[t]])
            P.dma("sp", dst[1].t.ap()[t * 128:(t + 1) * 128, :], acc[t][:], reads=[acc[t]], is_output=True)
        else:
            if t < NLAT:
                P.dma("sp", dst[1].t.ap()[t * 128:(t + 1) * 128, :], acc[t][:], reads=[acc[t]])
            else:
                P.dma("sp", dst[1].t.ap()[1024:1056, :], acc[t][0:32, :], reads=[acc[t]])


def build_fused(stop=None):
    nc = new_nc()
    P = Prog(nc)

    def ein(name, shape):
        if stop == "sel0" and (name in ("fing", "latin") or name[:-1] in ("wout", "w1", "w2") or name.endswith("1")):
            return None
        return P.dram(name, shape, F32, "ExternalInput")
    cT = ein("cT", [128, KT, 2])
    aw = ein("aw", [2, D, MODC])
    ab = ein("ab", [2, MODC])
    normg = ein("normg", [128, 2, 2, KT])
    fing = ein("fing", [D])
    xall = ein("xall", [NTALL, 128, D])
    latin = ein("latin", [NT, 128, D])
    L = []
    for l in range(2):
        L.append(dict(
            wsl=ein(f"wsl{l}", [D, MW]), gcw=ein(f"gcw{l}", [3, 384]), gsc=ein(f"gsc{l}", [4]), gng=ein(f"gng{l}", [128]),
            rmu=ein(f"rmu{l}", [RW]), rvec=ein(f"rvec{l}", [9, 128]), rw2=ein(f"rw2{l}", [2, 64, 128]),
            ra2=ein(f"ra2{l}", [2, 64, 128]), rg2=ein(f"rg2{l}", [160, 128]),
            wout=ein(f"wout{l}", [D, D]), w1=ein(f"w1{l}", [D, D_FF]), w2=ein(f"w2{l}", [D_FF, D])))
    if stop == "sel0":
        dbg = P.dram("dbg", [LROWS, D], F32, "ExternalOutput")
        dbgnv = P.dram("dbgnv", [128, 2, 8, KT], F32, "ExternalOutput")
    else:
        latout = P.dram("latout", [1024, D], F32, "ExternalOutput")
    modrow = P.dram("modrow", [2, 2, MODW], F32)
    modmine = P.dram("modmine", [128, 48], F32)
    modall = P.dram("modall", [NCORE * 128, 48], F32)
    Pg = P.dram("Pg", [GDN_ROWS, GW], F32)
    Pr = P.dram("Pr", [RW_ROWS, RW], F32)
    omine = P.dram("omine", [OROWS, 256], F32)
    oall = P.dram("oall", [NCORE * OROWS, 256], F32)
    latmine = P.dram("latmine", [1152, D], F32)
    oidx = P.dram("oidx", [128, NT * NCORE * 2], mybir.dt.int32, "ExternalInput")
    latall = P.dram("latall", [NCORE * LROWS, D], F32)
    NV = [P.sb("NV", [128, 8, KT]) for _ in range(2)]
    P.push_scope()
    emit_mod_shard(P, cT, aw, ab, normg, modmine, modall, modrow, NV)
    P.pop_scope()
    for l in range(2):
        W = L[l]
        P.push_scope()
        emit_proj_all(P, xall, None, W["wsl"], Pg, Pr, nv=NV[l], latall=(latall if l == 1 else None))
        P.pop_scope()
        P.push_scope()
        K = ScanConsts(P)
        emit_gdn(P, K, Pg, W["gcw"], W["gsc"], W["gng"], omine, ocol=0, tag=str(l), is_out=False)
        P.pop_scope()
        P.push_scope()
        K = ScanConsts(P)
        emit_rwkv(P, K, Pr, W["rmu"], W["rvec"], W["rw2"], W["ra2"], W["rg2"], omine, ocol=128, tag=str(l), is_out=False)
        P.pop_scope()
        P.collective("AllGather", omine.t.ap(), oall.t.ap())
        if stop == "sel0":
            for l2 in range(2):
                P.dma("sp", dbgnv.t.ap()[:, l2], NV[l2][:], reads=[NV[l2]], is_output=True)
            P.dma("sp", dbg.t.ap(), oall.t.ap()[0:LROWS * 8, :].rearrange("(a b) c -> a (b c)", b=8), is_output=True)
            break
        P.push_scope()
        emit_ffn(P, l, l == 1, NV[l], modrow, fing, oall, oidx,
                 ("tiles", latin) if l == 0 else ("rows", latmine), W["wout"], W["w1"], W["w2"],
                 ("rows", latmine) if l == 0 else ("out", latout))
        P.pop_scope()
        if l == 0:
            P.collective("AllGather", latmine.t.ap()[0:LROWS, :], latall.t.ap())
    P.emit()
    return nc


def make_oidx(r):
    idx = np.zeros((128, NT, NCORE, 2), np.int64)
    p = np.arange(128)
    e, c = p // 64, p % 64
    for t in range(NLAT):
        for i in range(NCORE):
            idx[:, t, i, 0] = 2 * (i * OROWS + 256 + r * 1024 + t * 128 + p)
            idx[:, t, i, 1] = 2 * (i * OROWS + 256 + c * 128 + 16 * r + 2 * t + e) + 1
    pp = np.minimum(p, 31)
    for i in range(NCORE):
        idx[:, NLAT, i, 0] = 2 * (i * OROWS + r * 32 + pp)
        idx[:, NLAT, i, 1] = 2 * (i * OROWS + r * 32 + pp) + 1
    return np.ascontiguousarray(idx.reshape(128, -1).astype(np.int32))


def fused_in_maps(inp):
    cT = np.stack([_featT(inp["c"][0]), _featT(inp["c_ctx"])], axis=-1).astype(np.float32)
    normg = np.stack([np.stack([_featT(inp["norm1_g"][l]), _featT(inp["norm2_g"][l])], axis=1) for l in range(2)],
                     axis=1).astype(np.float32)
    x = np.ascontiguousarray(inp["x"][0], dtype=np.float32)
    ctx = np.ascontiguousarray(inp["ctx"][0], dtype=np.float32)
    xall = np.ascontiguousarray(np.concatenate([ctx, x], axis=0).reshape(NTALL, 128, D))
    fing = np.ascontiguousarray(inp["final_g"], dtype=np.float32)
    shared = {"cT": cT, "normg": np.ascontiguousarray(normg), "fing": fing, "xall": xall}
    for l in range(2):
        shared[f"gng{l}"] = np.ascontiguousarray(inp["gdn_norm_g"][l])
        shared[f"wout{l}"] = np.ascontiguousarray(inp["w_out"][l])
        shared[f"w1{l}"] = np.ascontiguousarray(inp["mlp_w1"][l])
        shared[f"w2{l}"] = np.ascontiguousarray(inp["mlp_w2"][l])
    maps = []
    for i in range(NCORE):
        m = dict(shared)
        latin = np.zeros((NT, 128, D), np.float32)
        latin[:NLAT] = x[i * 1024:(i + 1) * 1024].reshape(NLAT, 128, D)
        latin[NLAT, :32] = ctx[i * 32:(i + 1) * 32]
        m["latin"] = latin
        m["oidx"] = make_oidx(i)
        msl = slice(i * MODC, (i + 1) * MODC)
        m["aw"] = np.ascontiguousarray(inp["ada_w"][:, :, msl], dtype=np.float32)
        m["ab"] = np.ascontiguousarray(inp["ada_b"][:, msl], dtype=np.float32)
        sl = slice(i * 128, (i + 1) * 128)
        cols = np.concatenate([gdn_cols(i), P_GDN + rwkv_cols(i)])
        for l in range(2):
            m[f"wsl{l}"] = np.ascontiguousarray(inp["w_in"][l][:, cols])
            m[f"gcw{l}"] = np.ascontiguousarray(inp["gdn_conv_w"][l][:, gdn_conv_cols(i)])
            m[f"gsc{l}"] = np.array([inp["gdn_a_log"][l][0, i], inp["gdn_a_log"][l][1, i], inp["gdn_dt_bias"][l][0, i],
                                     inp["gdn_dt_bias"][l][1, i]], np.float32)
            m[f"rmu{l}"] = np.ascontiguousarray(inp["rwkv_mu"][l][rwkv_cols(i)])
            m[f"rvec{l}"] = np.stack([inp["rwkv_w0"][l][0, sl], inp["rwkv_w0"][l][1, sl], inp["rwkv_a0"][l][0, sl],
                                      inp["rwkv_a0"][l][1, sl], inp["rwkv_k_k"][l][sl], inp["rwkv_k_a"][l][sl],
                                      inp["rwkv_r_k"][l][sl], inp["rwkv_ln_g"][l][sl],
                                      inp["rwkv_ln_b"][l][sl]]).astype(np.float32)
            m[f"rw2{l}"] = np.ascontiguousarray(inp["rwkv_w2"][l][:, :, sl])
            m[f"ra2{l}"] = np.ascontiguousarray(inp["rwkv_a2"][l][:, :, sl])
            m[f"rg2{l}"] = np.ascontiguousarray(inp["rwkv_g2"][l][:, sl])
        maps.append(m)
    return maps


def kernel(**inputs):
    inp = {k: np.asarray(v) for k, v in inputs.items()}
    if "fused" not in _PROGS:
        _PROGS["fused"] = build_fused()
    res = _run(_PROGS["fused"], fused_in_maps(inp))
    out = np.concatenate([res[i]["latout"] for i in range(NCORE)], axis=0)
    return out.reshape(1, 8192, D).astype(np.float32)
```
